# Optimizing a Trainium2 kernel written in Bass

```python
import jax, jax.numpy as jnp
from jax import lax
import numpy as np

D_MODEL = 2048
BATCH = 4
SEQ = 4096
DEPTH = 1

CONV_WIDTH = 3
CONV_GROUPS = 8
D_CONV = D_MODEL // 2
HGRN_HEADS = 8
HGRN_DK = 128
HGRN_DV = (D_MODEL // 2) // HGRN_HEADS
D_HGRN_K = HGRN_HEADS * HGRN_DK
D_HGRN_V = HGRN_HEADS * HGRN_DV
CHUNK = 64
D_FF = ((8 * D_MODEL // 3 + 255) // 256) * 256
N_MOD = 6
EPS = 1e-6
SPLITS = (D_CONV, D_CONV, D_CONV, D_HGRN_K, D_HGRN_K, D_HGRN_V, D_HGRN_V, D_MODEL, D_MODEL)
D_IN = sum(SPLITS)

kernel_name = "hybrid_conv_hgrn2_gated_merge_block"


def rmsnorm(x, g):
    xf = x.astype(jnp.float32)
    y = xf * lax.rsqrt(jnp.mean(xf * xf, axis=-1, keepdims=True) + EPS)
    return (y * g.astype(jnp.float32)).astype(x.dtype)


def causal_depthwise_conv(u, w):
    s = u.shape[1]
    upad = jnp.pad(u, ((0, 0), (CONV_WIDTH - 1, 0), (0, 0)))
    y = w[0] * upad[:, 0:s]
    for k in range(1, CONV_WIDTH):
        y = y + w[k] * upad[:, k:k + s]
    return y


def to_chunks(t):
    b, s, h, d = t.shape
    return t.reshape(b, s // CHUNK, CHUNK, h, d).transpose(1, 0, 3, 2, 4)


def from_chunks(t):
    n, b, h, c, d = t.shape
    return t.transpose(1, 0, 3, 2, 4).reshape(b, n * c, h, d)


def hgrn2_chunked(q, k, v, log_f):
    qc, kc, vc = to_chunks(q), to_chunks(k), to_chunks(v)
    bc = jnp.cumsum(to_chunks(log_f), axis=3)
    causal = jnp.tril(jnp.ones((CHUNK, CHUNK), dtype=bool))[:, :, None]
    b_, h_ = q.shape[0], q.shape[2]
    s0 = jnp.zeros((b_, h_, HGRN_DK, HGRN_DV), jnp.float32)

    def step(state, xs):
        qi, ki, vi, bi = xs
        b_last = bi[:, :, -1:, :]
        o_inter = jnp.einsum('bhtk,bhkv->bhtv', qi * jnp.exp(bi), state)
        diff = bi[:, :, :, None, :] - bi[:, :, None, :, :]
        decay = jnp.exp(jnp.where(causal, diff, -jnp.inf))
        scores = jnp.einsum('bhtk,bhtsk,bhsk->bhts', qi, decay, ki)
        o_intra = jnp.einsum('bhts,bhsv->bhtv', scores, vi)
        new_state = (jnp.swapaxes(jnp.exp(b_last), -1, -2) * state
                     + jnp.einsum('bhsk,bhsv->bhkv', ki * jnp.exp(b_last - bi), vi))
        return new_state, o_inter + o_intra

    _, o = lax.scan(step, s0, (qc, kc, vc, bc))
    return from_chunks(o)


def setup_inputs(seed: int = 0) -> dict:
    key = jax.random.key(seed)
    ks = jax.random.split(key, 20)

    def nrm(k, shape, fan_in):
        return jax.random.normal(k, shape, jnp.float32) * (fan_in ** -0.5)

    def gain(k, shape):
        return 1.0 + 0.02 * jax.random.normal(k, shape, jnp.float32)

    return {
        "x": jax.random.normal(ks[0], (BATCH, SEQ, D_MODEL), jnp.float32),
        "c": jax.random.normal(ks[1], (BATCH, D_MODEL), jnp.float32),
        "w_ada": nrm(ks[2], (DEPTH, D_MODEL, N_MOD * D_MODEL), D_MODEL) * 0.5,
        "b_ada": 0.02 * jax.random.normal(ks[3], (DEPTH, N_MOD * D_MODEL), jnp.float32),
        "norm_mix_g": gain(ks[4], (DEPTH, D_MODEL)),
        "w_in": nrm(ks[5], (DEPTH, D_MODEL, D_IN), D_MODEL),
        "conv_w": nrm(ks[6], (DEPTH, CONV_WIDTH, D_CONV), CONV_WIDTH),
        "lb_param": jax.random.normal(ks[7], (DEPTH + 1, D_HGRN_K), jnp.float32),
        "gnorm_g": gain(ks[8], (DEPTH, HGRN_DV)),
        "w_conv_out": nrm(ks[9], (DEPTH, D_CONV, D_MODEL), D_CONV),
        "w_hgrn_out": nrm(ks[10], (DEPTH, D_HGRN_V, D_MODEL), D_HGRN_V),
        "w_o": nrm(ks[11], (DEPTH, D_MODEL, D_MODEL), D_MODEL),
        "norm_ffn_g": gain(ks[12], (DEPTH, D_MODEL)),
        "w_ffn_gate": nrm(ks[13], (DEPTH, D_MODEL, D_FF), D_MODEL),
        "w_ffn_up": nrm(ks[14], (DEPTH, D_MODEL, D_FF), D_MODEL),
        "w_ffn_down": nrm(ks[15], (DEPTH, D_FF, D_MODEL), D_FF),
        "norm_final_g": gain(ks[16], (D_MODEL,)),
    }


def reference(x, c, w_ada, b_ada, norm_mix_g, w_in, conv_w, lb_param, gnorm_g,
              w_conv_out, w_hgrn_out, w_o, norm_ffn_g, w_ffn_gate, w_ffn_up, w_ffn_down,
              norm_final_g):
    b, s, _ = x.shape
    lb_all = jnp.cumsum(jax.nn.softmax(lb_param.astype(jnp.float32), axis=0), axis=0)
    split_idx = list(np.cumsum(SPLITS)[:-1])
    c_act = jax.nn.silu(c)

    for l in range(DEPTH):
        mod = c_act @ w_ada[l] + b_ada[l]
        sh_m, sc_m, gt_m, sh_f, sc_f, gt_f = [m[:, None, :] for m in jnp.split(mod, N_MOD, axis=-1)]

        h = rmsnorm(x, norm_mix_g[l]) * (1.0 + sc_m) + sh_m
        proj = h @ w_in[l]
        a_b, a_c, a_x, q, f_logit, i_in, g_out, gate_a, gate_b = jnp.split(proj, split_idx, axis=-1)

        y_a = (a_b * causal_depthwise_conv(a_c * a_x, conv_w[l])) @ w_conv_out[l]

        lb = lb_all[l]
        f = lb + (1.0 - lb) * jax.nn.sigmoid(f_logit.astype(jnp.float32))
        qh = jax.nn.silu(q.astype(jnp.float32)).reshape(b, s, HGRN_HEADS, HGRN_DK)
        kh = (1.0 - f).reshape(b, s, HGRN_HEADS, HGRN_DK)
        log_f = jnp.log(f).reshape(b, s, HGRN_HEADS, HGRN_DK)
        vh = i_in.astype(jnp.float32).reshape(b, s, HGRN_HEADS, HGRN_DV)
        o = hgrn2_chunked(qh, kh, vh, log_f)
        o = rmsnorm(o, gnorm_g[l]) * jax.nn.silu(g_out.astype(jnp.float32)).reshape(b, s, HGRN_HEADS, HGRN_DV)
        y_b = o.reshape(b, s, D_HGRN_V).astype(x.dtype) @ w_hgrn_out[l]

        merged = jax.nn.sigmoid(gate_a) * y_a + jax.nn.sigmoid(gate_b) * y_b
        x = x + gt_m * (merged @ w_o[l])

        h2 = rmsnorm(x, norm_ffn_g[l]) * (1.0 + sc_f) + sh_f
        ff = (jax.nn.silu(h2 @ w_ffn_gate[l]) * (h2 @ w_ffn_up[l])) @ w_ffn_down[l]
        x = x + gt_f * ff

    return rmsnorm(x, norm_final_g)
```

```python
import os
import numpy as np
from contextlib import ExitStack
import concourse.bass as bass
import concourse.mybir as mybir
from concourse.bass_utils import run_bass_kernel_spmd

F32 = mybir.dt.float32
BF16 = mybir.dt.bfloat16
AF = mybir.ActivationFunctionType
ALU = mybir.AluOpType

ENGS = ["pe", "act", "dve", "pool", "sp"]
SAME_ENGINE_SYNC = True
EPS = 1e-6
BIG = 3.0e38
DEBUG = os.environ.get("MK_DEBUG", "")


class StopProgram(Exception):
    pass


class Sched:
    def __init__(self):
        self.plan = False
        self.reset()

    def reset(self):
        self.streams = {e: [] for e in ENGS}
        self.cnt = {}
        self.seen = {e: {} for e in ENGS}
        self.lastw = {}
        self.readers = {}
        self.sem_names = ["c_" + e for e in ENGS if e != "sp"]

    def op(self, eng, fn, reads=(), writes=(), dsem=None):
        if self.plan:
            return None
        writes = list(writes) + [k for k in reads if k.startswith("ps") and k not in writes]
        need = {}

        def add(tok):
            if tok is None:
                return
            s, v = tok
            if need.get(s, 0) < v:
                need[s] = v

        for k in list(reads) + list(writes):
            add(self.lastw.get(k))
        for k in writes:
            for t in self.readers.get(k, ()):
                add(t)
        own = "c_" + eng
        waits = []
        for s, v in need.items():
            if s == own and (eng == "pe" or not SAME_ENGINE_SYNC):
                continue
            if self.seen[eng].get(s, 0) < v:
                waits.append((s, v))
                self.seen[eng][s] = v
        if dsem is not None:
            if dsem not in self.sem_names:
                self.sem_names.append(dsem)
            self.cnt[dsem] = self.cnt.get(dsem, 0) + 16
            tok = (dsem, self.cnt[dsem])
            inc = 16
        else:
            self.cnt[own] = self.cnt.get(own, 0) + 1
            tok = (own, self.cnt[own])
            inc = 1
        self.streams[eng].append((waits, fn, tok, inc))
        for k in reads:
            self.readers.setdefault(k, []).append(tok)
        for k in writes:
            self.lastw[k] = tok
            self.readers[k] = []
        return tok

    def wait_all(self, eng, keys):
        need = {}
        for k in keys:
            t = self.lastw.get(k)
            if t is not None and need.get(t[0], 0) < t[1]:
                need[t[0]] = t[1]
        self.streams[eng].append((list(need.items()), None, None, 0))

    def replay(self, block, sems):
        engobj = {"pe": "tensor", "act": "scalar", "dve": "vector", "pool": "gpsimd", "sp": "sync"}

        def run(ename):
            def body(eng):
                for waits, fn, tok, inc in self.streams[ename]:
                    for s, v in waits:
                        eng.wait_ge(sems[s], v)
                    if fn is None:
                        continue
                    inst = fn(eng)
                    inst.then_inc(sems[tok[0]], inc)
            return body

        for e in ENGS:
            if self.streams[e]:
                getattr(block, engobj[e])(run(e))


def build_nc(debug=""):
    nc = bass.Bass("TRN2", target_bir_lowering=False, dynamic_dma_scratch_size=4096)

    def din(name, shape):
        return nc.dram_tensor(name, shape, F32, kind="ExternalInput").ap()

    xo = din("xo", [2048, 2048])
    xp = din("xp", [2048, 2048])
    flag_d = din("flag", [128, 1])
    ccol_d = din("ccol", [128, 16])
    wada_d = din("wada", [32, 128, 6144])
    bada_d = din("badacol", [128, 96])
    gmix_d = din("gmix", [128, 16])
    gffn_d = din("gffn", [128, 16])
    gfin_d = din("gfin", [128, 16])
    convw_d = din("convw", [128, 24])
    lbp_d = din("lbp", [128, 16])
    gn_d = din("gn", [128, 1])
    wconv_d = din("w_conv", [8, 128, 6144])
    whq_d = din("w_hq", [8, 128, 6144])
    wf_d = din("w_f", [8, 128, 2048])
    wv_d = din("w_v", [4, 128, 4096])
    wgm_d = din("w_gm", [16, 128, 6144])
    wo_d = din("w_o", [8, 128, 4096])
    wgu_d = din("w_gu", [44, 128, 4096])
    wdn_d = din("w_dn", [16, 128, 5632])
    out_d = nc.dram_tensor("out", [2048, 2048], F32, kind="ExternalOutput").ap()
    dbg_d = None
    if debug:
        dbg_d = nc.dram_tensor("dbg", [128, 16384], F32, kind="ExternalOutput").ap()

    S = Sched()
    with ExitStack() as es:
        def sb(name, shape, dt):
            return es.enter_context(nc.sbuf_tensor(name, shape, dt))

        RX = sb("RX", [128, 16, 1024], F32)
        RH = sb("RH", [128, 16, 1024], BF16)
        RM = sb("RM", [128, 16, 1024], BF16)
        RU = sb("RU", [128, 16, 1024], BF16)
        WR = [sb(f"WR{i}", [128, 6144], BF16) for i in range(3)]
        HG = sb("HG", [128, 2560], BF16)
        ones_f = sb("ones_f", [128, 128], F32)
        ident_f = sb("ident_f", [128, 128], F32)
        ident_b = sb("ident_b", [128, 128], BF16)
        ones_b = sb("ones_b", [128, 128], BF16)
        smask = sb("smask", [128, 512], BF16)
        pmask = sb("pmask", [128, 4, 128], BF16)
        Sm = sb("Sm", [128, 8, 128], F32)
        Sp = sb("Sp", [128, 8, 128], BF16)
        Sh = sb("Sh", [128, 7, 128], F32)
        uhalo = sb("uhalo", [128, 8, 2], F32)
        tl2 = sb("tl2", [128, 2], F32)
        httail = sb("httail", [128, 16, 2], BF16)
        modcol = sb("modcol", [128, 96], F32)
        gmodm = sb("gmodm", [128, 16], F32)
        gmodf = sb("gmodf", [128, 16], F32)
        badac = sb("badac", [128, 96], F32)
        ccol = sb("ccol_s", [128, 16], F32)
        cact = sb("cact", [128, 16], BF16)
        gmix = sb("gmix_s", [128, 16], F32)
        gffn = sb("gffn_s", [128, 16], F32)
        gfin = sb("gfin_s", [128, 16], F32)
        convw = sb("convw_s", [128, 8, 3], F32)
        lbp = sb("lbp_s", [128, 8, 2], F32)
        gn = sb("gn_s", [128, 1], F32)
        flag = sb("flag_s", [128, 1], F32)
        lb = sb("lb", [128, 8], F32)
        oml = sb("oml", [128, 8], F32)
        noml = sb("noml", [128, 8], F32)
        em = sb("em", [128, 8], F32)
        dd = sb("dd", [128, 8], F32)
        epsc = sb("epsc", [128, 1], F32)
        ps = [es.enter_context(nc.psum_tensor(f"ps{i}", [128, 512], F32)) for i in range(8)]

        RMf = RM[:].rearrange("p k t -> p (k t)").bitcast(F32)
        RUf = RU[:].rearrange("p k t -> p (k t)").bitcast(F32)
        HGf = HG[:, 0:2048].bitcast(F32)

        def scr(i, n=512):
            return RMf[:, i * 512:i * 512 + n]

        def scrb(i):
            return RM[:, i, 0:512]

        Qp = HG[:, 0:512]
        Kp = HG[:, 512:1024]
        Ktok = HG[:, 1024:1536].rearrange("p (q k) -> p q k", q=4)
        Pm = HG[:, 1536:2048]
        sqo = HG[:, 2048:2560]

        def hs(h):
            return slice(h * 512, (h + 1) * 512)

        def RXk(kcs, h):
            return [f"RX{k}t{t}" for k in kcs for t in range(4 * h, 4 * h + 4)]

        def RHk(h):
            return [f"RH{k}h{h}" for k in range(16)]

        op = S.op

        def dstop(name):
            if debug == name:
                raise StopProgram()

        def ACT(out, in_, func, reads, writes, **kw):
            op("act", lambda e: e.activation(out=out, in_=in_, func=func, **kw), reads=reads, writes=writes)

        def TT(out, in0, in1, alu, reads, writes):
            op("dve", lambda e: e.tensor_tensor(out=out, in0=in0, in1=in1, op=alu), reads=reads, writes=writes)

        def PTT(out, in0, in1, alu, reads, writes):
            op("pool", lambda e: e.tensor_tensor(out=out, in0=in0, in1=in1, op=alu), reads=reads, writes=writes)

        def TS(out, in0, s1, s2, op0, op1, reads, writes):
            if s2 is None:
                op("dve", lambda e: e.tensor_scalar(out=out, in0=in0, scalar1=s1, scalar2=None, op0=op0),
                   reads=reads, writes=writes)
            else:
                op("dve", lambda e: e.tensor_scalar(out=out, in0=in0, scalar1=s1, scalar2=s2, op0=op0, op1=op1),
                   reads=reads, writes=writes)

        def STT(out, in0, scalar, in1, op0, op1, reads, writes):
            op("dve", lambda e: e.scalar_tensor_tensor(out=out, in0=in0, scalar=scalar, in1=in1, op0=op0, op1=op1),
               reads=reads, writes=writes)

        def MM(mms, reads, writes):
            mms = list(mms)

            def f(e):
                last = None
                for o, l, r, st, sp_ in mms:
                    last = e.matmul(o, lhsT=l, rhs=r, start=st, stop=sp_)
                return last
            op("pe", f, reads=reads, writes=writes)

        def TR(trs, reads, writes):
            trs = list(trs)

            def f(e):
                last = None
                for o, i, idn in trs:
                    last = e.transpose(out=o, in_=i, identity=idn)
                return last
            op("pe", f, reads=reads, writes=writes)

        wlist = []
        wst = {"i": 0, "iss": 0}

        def wload(s, src_k, n_k):
            op("pool", lambda e: e.dma_start(out=WR[s][:, 0:n_k], in_=src_k), writes=[f"W{s}"], dsem=f"d_w{s}")

        released = set()
        manual = set()

        def pump():
            lim = min(len(wlist), wst["i"] + 2)
            while wst["iss"] < lim:
                k = wst["iss"]
                if k >= 3 and (k - 3) not in released:
                    break
                if wlist[k][0] is not None:
                    wload(k % 3, wlist[k][0], wlist[k][1])
                wst["iss"] += 1

        def wtake(src, n, hold=0, man=False):
            if S.plan:
                wlist.append((src, n))
                wst["last"] = len(wlist) - 1
                return 0
            i = wst["i"]
            wst["i"] += 1
            wst["last"] = i
            if man:
                manual.add(i)
            for k in range(max(0, i - 4), i - hold):
                if k not in manual:
                    released.add(k)
            pump()
            assert wst["iss"] > i, ("weight tile not loadable (ring slot still held)", i)
            return i % 3

        def wdone(idx):
            if S.plan:
                return
            released.add(idx)
            pump()

        alt = {"i": 0}

        def COPY(out, in_, reads, writes, eng=None):
            if eng is None:
                alt["i"] += 1
                eng = "act" if alt["i"] % 2 else "dve"
            if eng == "act":
                ACT(out, in_, AF.Copy, reads, writes)
            else:
                op("dve", lambda e: e.tensor_copy(out=out, in_=in_), reads=reads, writes=writes)

        def DMA(eng, out, in_, reads, writes, dsem):
            op(eng, lambda e: e.dma_start(out=out, in_=in_), reads=reads, writes=writes, dsem=dsem)

        def W3(s, n, k):
            return WR[s][:, 0:n].rearrange("p (k c) -> p k c", k=k)

        def setup():
            smalls = [(flag[:], flag_d, "flag"), (ccol[:], ccol_d, "ccol"), (badac[:], bada_d, "badac"),
                      (gmix[:], gmix_d, "gmix"), (gffn[:], gffn_d, "gffn"), (gfin[:], gfin_d, "gfin"),
                      (convw[:].rearrange("p a b -> p (a b)"), convw_d, "convw"),
                      (lbp[:].rearrange("p a b -> p (a b)"), lbp_d, "lbp"), (gn[:], gn_d, "gn")]
            for t, d, k in smalls:
                DMA("sp", t, d, [], [k], "d_c")
            if not S.plan:
                last = S.lastw[smalls[-1][2]]
                for t, d, k in smalls:
                    S.lastw[k] = last

            def P(fn, reads, writes):
                op("pool", fn, reads=reads, writes=writes)
            P(lambda e: e.memset(ones_f[:], 1.0), [], ["ones_f"])
            P(lambda e: e.memset(ones_b[:], 1.0), [], ["ones_b"])
            P(lambda e: e.memset(epsc[:], EPS), [], ["epsc"])
            P(lambda e: e.memset(Sm[:], 0.0), [], [f"Sm{h}" for h in range(8)])
            P(lambda e: e.memset(uhalo[:], 0.0), [], ["uhalo"])
            P(lambda e: e.affine_select(out=ident_f[:], in_=ones_f[:], pattern=[[-1, 128]], compare_op=ALU.is_equal,
                                        fill=0.0, base=0, channel_multiplier=1), ["ones_f"], ["ident_f"])
            P(lambda e: e.tensor_copy(out=ident_b[:], in_=ident_f[:]), ["ident_f"], ["ident_b"])
            ones3 = ones_f[:, 0:64].unsqueeze(1).to_broadcast([128, 8, 64])
            P(lambda e: e.affine_select(out=smask[:].rearrange("p (c t) -> p c t", c=8), in_=ones3,
                                        pattern=[[0, 8], [1, 64]], compare_op=ALU.not_equal, fill=0.0, base=0,
                                        channel_multiplier=0), ["ones_f"], ["smask"])
            onesb = ones_f[:].unsqueeze(1).to_broadcast([128, 4, 128])
            P(lambda e: e.affine_select(out=pmask[:], in_=onesb, pattern=[[0, 4], [1, 128]], compare_op=ALU.is_ge,
                                        fill=0.0, base=0, channel_multiplier=-1), ["ones_f"], ["pmask"])
            P(lambda e: e.affine_select(out=pmask[:, :, 64:128], in_=pmask[:, :, 64:128], pattern=[[0, 4], [0, 64]],
                                        compare_op=ALU.is_ge, fill=0.0, base=-64, channel_multiplier=1),
              ["pmask"], ["pmask"])
            TT(lb[:], lbp[:, :, 0], lbp[:, :, 1], ALU.subtract, ["lbp"], ["lb"])
            ACT(lb[:], lb[:], AF.Sigmoid, ["lb"], ["lb"])
            TS(oml[:], lb[:], -1.0, 1.0, ALU.mult, ALU.add, ["lb"], ["oml"])
            TS(noml[:], lb[:], -1.0, None, ALU.add, None, ["lb"], ["noml"])
            ACT(cact[:], ccol[:], AF.Silu, ["ccol"], ["cact"])

        mod_pending = []

        def mod_some(n, banks=(0, 2)):
            k = min(n, len(mod_pending))
            if k:
                tiles = mod_pending[:k]
                del mod_pending[:k]
                mod(tiles, banks=banks)

        def mod(tiles, banks=(0, 1)):
            for t in tiles:
                s = wtake(wada_d[t], 6144)
                Wt = W3(s, 6144, 16)
                b = banks[t % 2]
                MM([(ps[b][:, g:g + 1], Wt[:, kc, g * 128:(g + 1) * 128], cact[:, kc:kc + 1], kc == 0, kc == 15)
                    for g in range(3) for kc in range(16)], [f"W{s}", "cact"], [f"ps{b}"])
                TT(modcol[:, 3 * t:3 * t + 3], ps[b][:, 0:3], badac[:, 3 * t:3 * t + 3], ALU.add,
                   [f"ps{b}", "badac"], ["modcol"])

        def mk_gmod(dst, g, key_g, c0, key):
            STT(dst[:], modcol[:, c0:c0 + 16], 1.0, g[:], ALU.add, ALU.mult, ["modcol", key_g], [key])

        def loadx(src, sbi, nmod=0):
            for tt in range(8):
                if nmod:
                    mod_some(nmod, banks=(6, 7))
                st = tt % 2
                xst = RUf[:, st * 2048:(st + 1) * 2048]
                skeys = [f"RU{4 * st + q}" for q in range(4)]
                r0 = sbi * 1024 + tt * 128
                DMA("sp", xst, src[r0:r0 + 128, :], [], skeys, f"d_x{st}")
                for g in range(4):
                    b = (tt * 4 + g) % 4
                    TR([(ps[b][:, q * 128:(q + 1) * 128], xst[:, (4 * g + q) * 128:(4 * g + q + 1) * 128], ident_f[:])
                        for q in range(4)], skeys + ["ident_f"], [f"ps{b}"])
                    COPY(RX[:, 4 * g:4 * g + 4, tt * 128:(tt + 1) * 128], ps[b][:].rearrange("p (q t) -> p q t", q=4),
                         [f"ps{b}"], [f"RX{4 * g + q}t{tt}" for q in range(4)])

        def rstd_half(h, dst, dkey, pb, sq_slots):
            for kc in range(16):
                sl = sq_slots[kc % 2]
                xin = RX[:, kc, hs(h)]
                if kc % 4 == 3:
                    ACT(scrb(sl), xin, AF.Square, RXk([kc], h), [f"RM{sl}"])
                else:
                    PTT(scrb(sl), xin, xin, ALU.mult, RXk([kc], h), [f"RM{sl}"])
                MM([(ps[pb][:], ones_b[:], scrb(sl), kc == 0, kc == 15)], [f"RM{sl}", "ones_b"], [f"ps{pb}"])
            ACT(dst, ps[pb][:], AF.Ln, [f"ps{pb}", "epsc"], [dkey], scale=1.0 / 2048.0, bias=epsc[:])
            ACT(dst, dst, AF.Exp, [dkey], [dkey], scale=-0.5)

        def norm_to_RH(gmod, sh_c0, gkey):
            for h in range(2):
                rs = scr(10 + h)
                rstd_half(h, rs, f"RM{10 + h}", 4 + h, (12, 13))
                for kc in range(16):
                    ts = 14 + kc % 2
                    TT(scr(ts), RX[:, kc, hs(h)], rs, ALU.mult, RXk([kc], h) + [f"RM{10 + h}"], [f"RM{ts}"])
                    ACT(RH[:, kc, hs(h)], scr(ts), AF.Identity, [f"RM{ts}", gkey, "modcol"], [f"RH{kc}h{h}"],
                        scale=gmod[:, kc:kc + 1], bias=modcol[:, sh_c0 + kc:sh_c0 + kc + 1])

        RXf = RX[:].rearrange("p k t -> p (k t)")
        conv_state = {}

        def conv_item(k, first=False):
            j, h = k // 2, k % 2
            so = 4 * (k % 2)
            def sl(i, n=512):
                return RXf[:, (so + i) * 1024:(so + i) * 1024 + n]
            def sk(i):
                return RXk([so + i], 0) + RXk([so + i], 1)
            t1, ub, y1, y2 = sl(0), sl(1, 514), sl(2), sl(3)
            T1, UB, Y1, Y2 = sk(0), sk(1), sk(2), sk(3)
            if h == 0:
                conv_state["s"] = wtake(wconv_d[j], 6144, man=True)
                conv_state["idx"] = wst["last"]
            s = conv_state["s"]
            Wt = W3(s, 6144, 16)
            if first and h == 0:
                MM([(ps[3][:, 2 * g:2 * g + 2], Wt[:, kc, (g + 1) * 128:(g + 2) * 128], httail[:, kc, :], kc == 0,
                     kc == 15) for g in range(2) for kc in range(16)], [f"W{s}", "httail"], ["ps3"])
                ACT(tl2[:], ps[3][:, 0:2], AF.Copy, ["ps3"], ["tl2"])
                STT(uhalo[:, j, :], ps[3][:, 2:4], flag[:, 0:1], tl2[:], ALU.mult, ALU.mult, ["ps3", "flag", "tl2"],
                    ["uhalo"])
            pb = 3
            MM([(ps[pb + g][:], Wt[:, kc, g * 128:(g + 1) * 128], RH[:, kc, hs(h)], kc == 0, kc == 15)
                for g in range(3) for kc in range(16)], [f"W{s}"] + RHk(h), [f"ps{pb}", f"ps{pb + 1}", f"ps{pb + 2}"])
            if h == 1:
                wdone(conv_state["idx"])
            ACT(t1, ps[pb + 1][:], AF.Copy, [f"ps{pb + 1}"], T1)
            ACT(ub[:, 0:2], uhalo[:, j, :], AF.Copy, ["uhalo"], UB)
            TT(ub[:, 2:514], ps[pb + 2][:], t1, ALU.mult, [f"ps{pb + 2}"] + T1, UB)
            ACT(uhalo[:, j, :], ub[:, 512:514], AF.Copy, UB, ["uhalo"])
            TS(y1, ub[:, 0:512], convw[:, j, 0:1], None, ALU.mult, None, UB + ["convw"], Y1)
            STT(y2, ub[:, 1:513], convw[:, j, 1:2], y1, ALU.mult, ALU.add, UB + ["convw"] + Y1, Y2)
            STT(y1, ub[:, 2:514], convw[:, j, 2:3], y2, ALU.mult, ALU.add, UB + ["convw"] + Y2, Y1)
            TT(RU[:, j, hs(h)], ps[pb][:], y1, ALU.mult, [f"ps{pb}"] + Y1, [f"RU{j}"])

        def reloadx_start(src, sbi, tt):
            s = wtake(None, 4096, man=True)
            idx = wst["last"]
            stage = WR[s][:, 0:4096].bitcast(F32)
            r0 = sbi * 1024 + tt * 128
            DMA("sp", stage, src[r0:r0 + 128, :], [], [f"W{s}"], f"d_w{s}")
            return (s, idx, stage, tt)

        def reloadx_finish(st, bank0=4):
            s, idx, stage, tt = st
            for g in range(4):
                b = bank0 + g
                TR([(ps[b][:, q * 128:(q + 1) * 128], stage[:, (4 * g + q) * 128:(4 * g + q + 1) * 128], ident_f[:])
                    for q in range(4)], [f"W{s}", "ident_f"], [f"ps{b}"])
                COPY(RX[:, 4 * g:4 * g + 4, tt * 128:(tt + 1) * 128], ps[b][:].rearrange("p (q t) -> p q t", q=4),
                     [f"ps{b}"], [f"RX{4 * g + q}t{tt}" for q in range(4)])
            wdone(idx)

        def conv_tail():
            for j in range(8):
                s = wtake(wconv_d[j], 6144)
                Wt = W3(s, 6144, 16)
                b = 6 + j % 2
                MM([(ps[b][:, 2 * g:2 * g + 2], Wt[:, kc, (g + 1) * 128:(g + 2) * 128], RH[:, kc, 1022:1024], kc == 0,
                     kc == 15) for g in range(2) for kc in range(16)], [f"W{s}"] + RHk(1), [f"ps{b}"])
                ACT(tl2[:], ps[b][:, 0:2], AF.Copy, [f"ps{b}"], ["tl2"])
                STT(uhalo[:, j, :], ps[b][:, 2:4], flag[:, 0:1], tl2[:], ALU.mult, ALU.mult, [f"ps{b}", "flag", "tl2"],
                    ["uhalo"])

        def vproj(prefix):
            for cb in range(2):
                s0 = wtake(wv_d[cb * 2], 4096)
                s1 = wtake(wv_d[cb * 2 + 1], 4096, hold=1)
                Ws = [W3(s0, 4096, 8), W3(s1, 4096, 8)]
                for tt in range(8):
                    b = 3 + (cb * 8 + tt) % 4
                    MM([(ps[b][:], RH[:, kc, tt * 128:(tt + 1) * 128], Ws[kc // 8][:, kc % 8, :], kc == 0, kc == 15)
                        for kc in range(16)], [f"W{s0}", f"W{s1}"] + RHk(tt // 4), [f"ps{b}"])
                    dst = RM[:, tt, cb * 512:(cb + 1) * 512]
                    if prefix:
                        ACT(dst, ps[b][:], AF.Copy, [f"ps{b}", "flag"], [f"RM{tt}"], scale=flag[:, 0:1])
                    else:
                        COPY(dst, ps[b][:], [f"ps{b}"], [f"RM{tt}"])

        def hgrn(prefix, first=False):
            items = [(hd, h) for hd in range(8) for h in range(2)]
            n_it = len(items)
            wslot = {}
            widx = {}
            L = 2 if prefix else 1
            t = [scr(8 + k) for k in range(8)]
            K = [f"RM{8 + k}" for k in range(8)]
            X, Y = 6, 7
            psXb = ps[X][:].bitcast(BF16)
            bc3 = t[2].rearrange("p (c t) -> p c t", c=8)
            bm3 = t[6].rearrange("p (c t) -> p c t", c=8)

            def bigpart(i, part):
                if i >= n_it:
                    return
                hd, h = items[i]
                s3 = 3 * (i % 2) if prefix else 0
                if part == 0 and h == 0:
                    if prefix:
                        wslot[hd] = wtake(wf_d[hd], 2048)
                    else:
                        wslot[hd] = wtake(whq_d[hd], 6144, man=True)
                        widx[hd] = wst["last"]
                s = wslot[hd]
                if prefix:
                    if part != 0:
                        return
                    Wt = W3(s, 2048, 16)
                    MM([(ps[s3 + 1][:], Wt[:, kc, :], RH[:, kc, hs(h)], kc == 0, kc == 15) for kc in range(16)],
                       [f"W{s}"] + RHk(h), [f"ps{s3 + 1}"])
                else:
                    Wt = W3(s, 6144, 16)
                    g = part
                    MM([(ps[s3 + g][:], Wt[:, kc, g * 128:(g + 1) * 128], RH[:, kc, hs(h)], kc == 0, kc == 15)
                        for kc in range(16)], [f"W{s}"] + RHk(h), [f"ps{s3 + g}"])
                    if part == 2 and h == 1:
                        wdone(widx[hd])

            def elem(i):
                hd, h = items[i]
                s3 = 3 * (i % 2) if prefix else 0
                pq, pf, pg = ps[s3], ps[s3 + 1], ps[s3 + 2]
                ACT(t[0], pf[:], AF.Sigmoid, [f"ps{s3 + 1}"], [K[0]])
                if not prefix:
                    ACT(t[4], pq[:], AF.Silu, [f"ps{s3}"], [K[4]])
                    ACT(t[5], pg[:], AF.Silu, [f"ps{s3 + 2}"], [K[5]])
                ACT(t[1], t[0], AF.Ln, [K[0], "oml", "lb"], [K[1]], scale=oml[:, hd:hd + 1], bias=lb[:, hd:hd + 1])
                op("dve", lambda e: e.tensor_tensor_scan(out=t[2], data0=smask[:], data1=t[1], initial=0.0,
                                                         op0=ALU.mult, op1=ALU.add),
                   reads=[K[1], "smask"], writes=[K[2]])
                TS(t[3], t[0], noml[:, hd:hd + 1], oml[:, hd:hd + 1], ALU.mult, ALU.add, [K[0], "noml", "oml"], [K[3]])
                ACT(dd[:], bc3[:, :, 63], AF.Exp, [K[2]], ["dd"])
                if not prefix:
                    ACT(em[:], bc3[:, :, 32], AF.Exp, [K[2]], ["em"])
                TT(bm3, bc3, bc3[:, :, 32:33].to_broadcast([128, 8, 64]), ALU.subtract, [K[2]], [K[6]])
                ACT(t[1], t[6], AF.Exp, [K[6]], [K[1]], scale=-1.0)
                if prefix:
                    ACT(bm3[:, :, 63], bm3[:, :, 63], AF.Exp, [K[6]], [K[6]])
                else:
                    ACT(t[6], t[6], AF.Exp, [K[6]], [K[6]])
                if not prefix:
                    TT(Qp, t[4], t[6], ALU.mult, [K[4], K[6]], ["Qp"])
                TT(Kp, t[3], t[1], ALU.mult, [K[3], K[1]], ["Kp"])

            def small(i):
                hd, h = items[i]
                TR([(psXb[:, q * 128:(q + 1) * 128], Kp[:, q * 128:(q + 1) * 128], ident_b[:]) for q in range(4)],
                   ["Kp", "ident_b"], [f"ps{X}"])
                ACT(Ktok, psXb[:, 0:512].rearrange("p (q k) -> p q k", q=4), AF.Copy, [f"ps{X}"], ["Ktok"])
                if not prefix:
                    MM([(ps[Y][:, q * 128:(q + 1) * 128], Kp[:, q * 128:(q + 1) * 128], Qp[:, q * 128:(q + 1) * 128],
                         True, True) for q in range(4)], ["Kp", "Qp"], [f"ps{Y}"])
                    STT(Pm, ps[Y][:], BIG, pmask[:].rearrange("p a b -> p (a b)"), ALU.min, ALU.mult,
                        [f"ps{Y}", "pmask"], ["Pm"])
                    dstop("s1")
                bigpart(i + L, 0)
                vt = [f"RM{4 * h + q}" for q in range(4)]
                AB = [X, Y]
                MM([(ps[AB[c % 2]][:, (c // 2) * 128:(c // 2 + 1) * 128],
                     Ktok[(c % 2) * 64:(c % 2) * 64 + 64, c // 2, :],
                     RM[(c % 2) * 64:(c % 2) * 64 + 64, 4 * h + c // 2, hd * 128:(hd + 1) * 128], True, True)
                    for c in range(8)], ["Ktok"] + vt, [f"ps{X}", f"ps{Y}"])
                bigpart(i + L, 1)
                t6q = t[6].rearrange("p (q two t) -> p q two t", two=2, t=64)
                for par in range(2):
                    pv = ps[AB[par]][:].rearrange("p (q v) -> p q v", q=4)
                    TT(pv, pv, t6q[:, :, par, 63:64].to_broadcast([128, 4, 128]), ALU.mult,
                       [f"ps{AB[par]}", K[6]], [f"ps{AB[par]}"])
                for c in range(8):
                    srcS = Sm[:, hd, :] if c == 0 else Sh[:, c - 1, :]
                    skey = f"Sm{hd}" if c == 0 else f"Sh{c - 1}"
                    dstS = Sm[:, hd, :] if c == 7 else Sh[:, c, :]
                    dkey = f"Sm{hd}" if c == 7 else f"Sh{c}"
                    if not prefix:
                        ACT(Sp[:, c, :], srcS, AF.Copy, [skey, "em"], [f"Sp{c}"], scale=em[:, c:c + 1])
                    STT(dstS, srcS, dd[:, c:c + 1], ps[AB[c % 2]][:, (c // 2) * 128:(c // 2 + 1) * 128],
                        ALU.mult, ALU.add, [skey, "dd", f"ps{AB[c % 2]}"], [dkey])
                if prefix:
                    bigpart(i + L, 2)
                    return
                dstop("s2")
                mms = []
                for q in range(4):
                    mms.append((ps[X][:, q * 128:(q + 1) * 128], RM[:, 4 * h + q, hd * 128:(hd + 1) * 128],
                                Pm[:, q * 128:(q + 1) * 128], True, False))
                    for cc in range(2):
                        c = 2 * q + cc
                        mms.append((ps[X][:, c * 64:(c + 1) * 64], Sp[:, c, :], Qp[:, c * 64:(c + 1) * 64], False,
                                    cc == 1))
                MM(mms, vt + ["Pm", "Qp"] + [f"Sp{c}" for c in range(8)], [f"ps{X}"])
                bigpart(i + L, 2)
                dstop("s3")
                ACT(sqo, ps[X][:], AF.Square, [f"ps{X}"], ["sqo"])
                TT(t[7], ps[X][:], t[5], ALU.mult, [f"ps{X}", K[5]], [K[7]])
                MM([(ps[Y][:], ones_b[:], sqo, True, True)], ["sqo", "ones_b"], [f"ps{Y}"])
                ACT(t[0], ps[Y][:], AF.Ln, [f"ps{Y}", "epsc"], [K[0]], scale=1.0 / 128.0, bias=epsc[:])
                ACT(t[0], t[0], AF.Exp, [K[0]], [K[0]], scale=-0.5)
                STT(RU[:, 8 + hd, hs(h)], t[7], gn[:, 0:1], t[0], ALU.mult, ALU.mult, [K[7], "gn", K[0]],
                    [f"RU{8 + hd}"])

            for i0 in range(L):
                for part in range(3):
                    bigpart(i0, part)
            for i in range(n_it):
                elem(i)
                if prefix and items[i][1] == 0:
                    mod_some(prefix)
                if not prefix:
                    conv_item(i, first)
                small(i)
                if not prefix:
                    dstop("s4")

        def hgrn_prefix(nmod):
            def rxs(n, w=512):
                return RXf[:, n * 1024:n * 1024 + w], RXk([n], 0) + RXk([n], 1)
            pipes = []
            for p in range(2):
                if p == 0:
                    tl = [(scr(8 + k), [f"RM{8 + k}"]) for k in range(5)]
                    bs = dict(t=tl, Kp=(Kp, ["Kp"]), Ktok=(Ktok, ["Ktok"]), dd=(dd, ["dd"]),
                              Sh=(Sh, [f"Sh{c}" for c in range(7)]), F=1, X=6, Y=7)
                else:
                    tl = [rxs(k) for k in range(5)]
                    shv, shk = rxs(5, 896)
                    bs = dict(t=tl, Kp=(Qp, ["Qp"]), Ktok=(HG[:, 1536:2048].rearrange("p (q k) -> p q k", q=4), ["Pm"]),
                              dd=(em, ["em"]), Sh=(shv.rearrange("p (c v) -> p c v", c=7), shk), F=4, X=3, Y=5)
                bs["heads"] = [0, 2, 4, 6] if p == 0 else [1, 3, 5, 7]
                pipes.append(bs)

            def proj(bs, hd, h):
                if h == 0:
                    bs["s"] = wtake(wf_d[hd], 2048, man=True)
                    bs["idx"] = wst["last"]
                s = bs["s"]
                Wt = W3(s, 2048, 16)
                F = bs["F"]
                MM([(ps[F][:], Wt[:, kc, :], RH[:, kc, hs(h)], kc == 0, kc == 15) for kc in range(16)],
                   [f"W{s}"] + RHk(h), [f"ps{F}"])
                if h == 1:
                    wdone(bs["idx"])

            def st1(bs, hd, h):
                (sg, ksg), (lf, klf), (bc, kbc), (kk, kkk), (bm, kbm) = bs["t"]
                F = bs["F"]
                ACT(sg, ps[F][:], AF.Sigmoid, [f"ps{F}"], ksg)
                ACT(lf, sg, AF.Ln, ksg + ["oml", "lb"], klf, scale=oml[:, hd:hd + 1], bias=lb[:, hd:hd + 1])
                op("dve", lambda e: e.tensor_tensor_scan(out=bc, data0=smask[:], data1=lf, initial=0.0,
                                                         op0=ALU.mult, op1=ALU.add),
                   reads=klf + ["smask"], writes=kbc)
                TS(kk, sg, noml[:, hd:hd + 1], oml[:, hd:hd + 1], ALU.mult, ALU.add, ksg + ["noml", "oml"], kkk)
                bc3 = bc.rearrange("p (c t) -> p c t", c=8)
                ddv, kdd = bs["dd"]
                ACT(ddv[:], bc3[:, :, 63], AF.Exp, kbc, kdd)

            def st2(bs, hd, h):
                (sg, ksg), (lf, klf), (bc, kbc), (kk, kkk), (bm, kbm) = bs["t"]
                bc3 = bc.rearrange("p (c t) -> p c t", c=8)
                bm3 = bm.rearrange("p (c t) -> p c t", c=8)
                TT(bm3, bc3, bc3[:, :, 32:33].to_broadcast([128, 8, 64]), ALU.subtract, kbc, kbm)
                ACT(lf, bm, AF.Exp, kbm, klf, scale=-1.0)
                ACT(bm3[:, :, 63], bm3[:, :, 63], AF.Exp, kbm, kbm)
                kpv, kkp = bs["Kp"]
                TT(kpv, kk, lf, ALU.mult, kkk + klf, kkp)

            def st3(bs, hd, h):
                kpv, kkp = bs["Kp"]
                ktv, kkt = bs["Ktok"]
                X = bs["X"]
                psXb_ = ps[X][:].bitcast(BF16)
                TR([(psXb_[:, q * 128:(q + 1) * 128], kpv[:, q * 128:(q + 1) * 128], ident_b[:]) for q in range(4)],
                   kkp + ["ident_b"], [f"ps{X}"])
                ACT(ktv, psXb_[:, 0:512].rearrange("p (q k) -> p q k", q=4), AF.Copy, [f"ps{X}"], kkt)

            def st4(bs, hd, h):
                (sg, ksg), (lf, klf), (bc, kbc), (kk, kkk), (bm, kbm) = bs["t"]
                ktv, kkt = bs["Ktok"]
                AB = [bs["X"], bs["Y"]]
                vt = [f"RM{4 * h + q}" for q in range(4)]
                MM([(ps[AB[c % 2]][:, (c // 2) * 128:(c // 2 + 1) * 128],
                     ktv[(c % 2) * 64:(c % 2) * 64 + 64, c // 2, :],
                     RM[(c % 2) * 64:(c % 2) * 64 + 64, 4 * h + c // 2, hd * 128:(hd + 1) * 128], True, True)
                    for c in range(8)], kkt + vt, [f"ps{AB[0]}", f"ps{AB[1]}"])
                bmq = bm.rearrange("p (q two t) -> p q two t", two=2, t=64)
                for par in range(2):
                    pv = ps[AB[par]][:].rearrange("p (q v) -> p q v", q=4)
                    TT(pv, pv, bmq[:, :, par, 63:64].to_broadcast([128, 4, 128]), ALU.mult,
                       [f"ps{AB[par]}"] + kbm, [f"ps{AB[par]}"])

            def st5(bs, hd, h):
                AB = [bs["X"], bs["Y"]]
                shv, shk = bs["Sh"]
                ddv, kdd = bs["dd"]
                for c in range(8):
                    srcS = Sm[:, hd, :] if c == 0 else shv[:, c - 1, :]
                    skey = [f"Sm{hd}"] if c == 0 else shk
                    dstS = Sm[:, hd, :] if c == 7 else shv[:, c, :]
                    dkey = [f"Sm{hd}"] if c == 7 else shk
                    STT(dstS, srcS, ddv[:, c:c + 1], ps[AB[c % 2]][:, (c // 2) * 128:(c // 2 + 1) * 128],
                        ALU.mult, ALU.add, skey + kdd + [f"ps{AB[c % 2]}"], dkey)

            seq = [[(hd, h) for hd in bs["heads"] for h in range(2)] for bs in pipes]
            for p in range(2):
                proj(pipes[p], *seq[p][0])
            for n in range(8):
                for stage in (st1, st2, st3, st4, st5):
                    for p in range(2):
                        stage(pipes[p], *seq[p][n])
                    if stage is st1 and n + 1 < 8:
                        for p in range(2):
                            proj(pipes[p], *seq[p][n + 1])
                        if n % 2 == 0:
                            mod_some(nmod)

        def gate(sbi):
            pend = []
            sa = HG[:, 1024:1536]
            sbb = HG[:, 1536:2048]
            m1 = HGf[:, 0:512]
            for j in range(16):
                if j % 2 == 0:
                    pend.append(reloadx_start(xo, sbi, j // 2))
                s = wtake(wgm_d[j], 6144, man=True)
                gidx = wst["last"]
                Wg = WR[s][:, 0:4096].rearrange("p (k c) -> p k c", k=16)
                Wy = WR[s][:, 4096:6144].rearrange("p (k c) -> p k c", k=16)
                for h in range(2):
                    pb = 4 * ((j * 2 + h) % 2)
                    mms = [(ps[pb + g][:], Wg[:, kc, g * 128:(g + 1) * 128], RH[:, kc, hs(h)], kc == 0, kc == 15)
                           for g in range(2) for kc in range(16)]
                    mms += [(ps[pb + 2 + g][:], Wy[:, 8 * g + kc, :], RU[:, 8 * g + kc, hs(h)], kc == 0, kc == 7)
                            for g in range(2) for kc in range(8)]
                    MM(mms, [f"W{s}"] + RHk(h) + [f"RU{k}" for k in range(16)], [f"ps{pb + g}" for g in range(4)])
                    if h == 1:
                        wdone(gidx)
                    ACT(sa, ps[pb][:], AF.Sigmoid, [f"ps{pb}"], ["Ktok"])
                    ACT(sbb, ps[pb + 1][:], AF.Sigmoid, [f"ps{pb + 1}"], ["Pm"])
                    TT(m1, ps[pb + 2][:], sa, ALU.mult, [f"ps{pb + 2}", "Ktok"], ["Qp", "Kp"])
                    TT(ps[pb][:], ps[pb + 3][:], sbb, ALU.mult, [f"ps{pb + 3}", "Pm"], [f"ps{pb}"])
                    TT(RM[:, j, hs(h)], ps[pb][:], m1, ALU.add, [f"ps{pb}", "Qp", "Kp"], [f"RM{j}"])
                    if h == 0 and pend:
                        reloadx_finish(pend.pop(0))
            while pend:
                reloadx_finish(pend.pop(0))

        cntr = {"b": 0}

        def wo():
            for jp in range(8):
                s = wtake(wo_d[jp], 4096)
                Wt = W3(s, 4096, 16)
                for jl in range(2):
                    jc = 2 * jp + jl
                    for h in range(2):
                        b = cntr["b"] % 8
                        cntr["b"] += 1
                        MM([(ps[b][:], Wt[:, kc, jl * 128:(jl + 1) * 128], RM[:, kc, hs(h)], kc == 0, kc == 15)
                            for kc in range(16)], [f"W{s}"] + [f"RM{k}" for k in range(16)], [f"ps{b}"])
                        rk = RXk([jc], h)
                        STT(RX[:, jc, hs(h)], ps[b][:], modcol[:, 32 + jc:33 + jc], RX[:, jc, hs(h)], ALU.mult, ALU.add,
                            [f"ps{b}", "modcol"] + rk, rk)

        def ffn():
            sgb = [HGf[:, 0:512], HGf[:, 512:1024]]
            sgk = [["Qp", "Kp"], ["Ktok", "Pm"]]
            cnt = 0
            for qq in range(4):
                for jl in range(11):
                    jj = qq * 11 + jl
                    s = wtake(wgu_d[jj], 4096)
                    Wt = W3(s, 4096, 16)
                    for h in range(2):
                        pb = 2 * (cnt % 4)
                        sg = sgb[cnt % 2]
                        sk = sgk[cnt % 2]
                        cnt += 1
                        MM([(ps[pb + g][:], Wt[:, kc, g * 128:(g + 1) * 128], RH[:, kc, hs(h)], kc == 0, kc == 15)
                            for g in range(2) for kc in range(16)], [f"W{s}"] + RHk(h), [f"ps{pb}", f"ps{pb + 1}"])
                        ACT(sg, ps[pb][:], AF.Silu, [f"ps{pb}"], sk)
                        TT(RM[:, jl, hs(h)], ps[pb + 1][:], sg, ALU.mult, [f"ps{pb + 1}"] + sk, [f"RM{jl}"])
                for cbd in range(4):
                    s = wtake(wdn_d[qq * 4 + cbd], 5632)
                    Wt = W3(s, 5632, 11)
                    for jl4 in range(4):
                        jc = cbd * 4 + jl4
                        for h in range(2):
                            b = cnt % 8
                            cnt += 1
                            MM([(ps[b][:], Wt[:, kc, jl4 * 128:(jl4 + 1) * 128], RM[:, kc, hs(h)], kc == 0, kc == 10)
                                for kc in range(11)], [f"W{s}"] + [f"RM{k}" for k in range(11)], [f"ps{b}"])
                            rk = RXk([jc], h)
                            STT(RX[:, jc, hs(h)], ps[b][:], modcol[:, 80 + jc:81 + jc], RX[:, jc, hs(h)], ALU.mult,
                                ALU.add, [f"ps{b}", "modcol"] + rk, rk)

        def final(sbi):
            for h in range(2):
                rstd_half(h, scr(h), f"RM{h}", 4 + h, (2, 3))
            for tt in range(8):
                h = tt // 4
                tb = 8 + 4 * (tt % 2)
                tmp = RMf[:, tb * 512:(tb + 4) * 512].rearrange("p (k t) -> p k t", k=16)
                tk = [f"RM{tb + q}" for q in range(4)]
                rs = scr(h)[:, (tt % 4) * 128:(tt % 4 + 1) * 128]
                for kc in range(16):
                    STT(tmp[:, kc, :], RX[:, kc, tt * 128:(tt + 1) * 128], gfin[:, kc:kc + 1], rs, ALU.mult, ALU.mult,
                        [f"RX{kc}t{tt}", "gfin", f"RM{h}"], [tk[kc // 4]])
                st = tt % 4
                ost = RUf[:, st * 2048:(st + 1) * 2048]
                okeys = [f"RU{4 * st + q}" for q in range(4)]
                for g in range(4):
                    b = (tt * 4 + g) % 4
                    TR([(ps[b][:, q * 128:(q + 1) * 128], tmp[:, 4 * g + q, :], ident_f[:]) for q in range(4)],
                       [tk[g], "ident_f"], [f"ps{b}"])
                    COPY(ost[:, g * 512:(g + 1) * 512], ps[b][:], [f"ps{b}"], [okeys[g]], eng="act")
                r0 = sbi * 1024 + tt * 128
                DMA("sp", out_d[r0:r0 + 128, :], ost, okeys, [f"out{sbi}_{tt}"], f"d_o{st}")

        def dump(ap2d, keys, n):
            op("sp", lambda e: e.dma_start(out=dbg_d[:, 0:n], in_=ap2d), reads=keys, writes=["dbg"], dsem="d_dbg")

        def program():
            setup()
            mod(range(0, 11))
            mk_gmod(gmodm, gmix, "gmix", 16, "gmodm")
            loadx(xp, 0)
            norm_to_RH(gmodm, 0, "gmodm")
            mod_pending.extend(range(11, 32))
            vproj(True)
            hgrn_prefix(2)
            loadx(xp, 1, nmod=1)
            norm_to_RH(gmodm, 0, "gmodm")
            ACT(httail[:], RH[:, :, 1022:1024], AF.Copy, RHk(1), ["httail"])
            vproj(True)
            hgrn_prefix(1)
            mod_some(32)
            mk_gmod(gmodf, gffn, "gffn", 64, "gmodf")
            for sbi in range(2):
                loadx(xo, sbi)
                norm_to_RH(gmodm, 0, "gmodm")
                if debug == "h" and sbi == 0:
                    return
                vproj(False)
                hgrn(False, first=(sbi == 0))
                if debug == "u" and sbi == 0:
                    return
                gate(sbi)
                if debug == "m" and sbi == 0:
                    return
                wo()
                if debug == "x1" and sbi == 0:
                    return
                norm_to_RH(gmodf, 48, "gmodf")
                ffn()
                if debug == "x2" and sbi == 0:
                    return
                final(sbi)

        S.plan = True
        try:
            program()
        except StopProgram:
            pass
        S.plan = False
        S.reset()
        try:
            program()
        except StopProgram:
            pass
        if debug:
            if debug == "h":
                op("act", lambda e: e.activation(out=RX[:, 0:8, :], in_=RH[:, 0:8, :], func=AF.Copy),
                   reads=RHk(0) + RHk(1), writes=RXk(range(8), 0) + RXk(range(8), 1))
            if debug in ("u", "c", "v", "s0", "s1", "s2", "s4", "s3a", "s3b", "s3c", "s3x"):
                op("act", lambda e: e.activation(out=RX[:, :, :], in_=RU[:, :, :], func=AF.Copy),
                   reads=[f"RU{k}" for k in range(16)], writes=RXk(range(16), 0) + RXk(range(16), 1))
            if debug == "s3":
                t = [scr(8 + k) for k in range(8)]
                srcs = [(ps[7][:], ["ps7"]), (Qp, ["Qp"]), (Kp, ["Kp"]), (Pm, ["Pm"]), (t[2], ["RM10"]), (t[6], ["RM14"]),
                        (t[1], ["RM9"]), (t[3], ["RM11"]), (t[4], ["RM12"]), (ps[5][:], ["ps5"]), (ps[4][:], ["ps4"])]
                for n_, (a_, k_) in enumerate(srcs):
                    ACT(RX[:, n_, 0:512], a_, AF.Copy, k_, RXk([n_], 0))
                ACT(RX[:, 11, 0:1024], Sm[:].rearrange("p a b -> p (a b)"), AF.Copy, [f"Sm{h}" for h in range(8)], RXk([11], 0) + RXk([11], 1))
            if debug == "m":
                op("act", lambda e: e.activation(out=RX[:, :, :], in_=RM[:, :, :], func=AF.Copy),
                   reads=[f"RM{k}" for k in range(16)], writes=RXk(range(16), 0) + RXk(range(16), 1))
            dump(RX[:].rearrange("p k t -> p (k t)"), RXk(range(16), 0) + RXk(range(16), 1), 16384)
            S.wait_all("sp", ["dbg"])
        else:
            S.wait_all("sp", [f"out{sbi}_{tt}" for sbi in range(2) for tt in range(8)])
        sems = {n: es.enter_context(nc.semaphore(n)) for n in S.sem_names}
        with nc.Block() as block:
            S.replay(block, sems)
    return nc


def _tile_cols(W, col_lists):
    K = W.shape[0]
    kc = K // 128
    Wr = W.reshape(kc, 128, W.shape[1])
    out = []
    for cols in col_lists:
        t = Wr[:, :, cols]
        out.append(np.ascontiguousarray(t.transpose(1, 0, 2)).reshape(128, -1))
    return np.stack(out)


def _col(v):
    return np.ascontiguousarray(v.reshape(-1, 128).T).astype(np.float32)


_CACHE = {}


def _prep_shared(inp):
    D = 2048
    w_in = inp["w_in"][0]
    ar = np.arange
    sh = {}
    wada = inp["w_ada"][0]
    sh["wada"] = _tile_cols(wada, [ar(t * 384, (t + 1) * 384) for t in range(32)])
    sh["gmix"] = _col(inp["norm_mix_g"][0])
    sh["gffn"] = _col(inp["norm_ffn_g"][0])
    sh["gfin"] = _col(inp["norm_final_g"])
    cw = inp["conv_w"][0]
    sh["convw"] = np.ascontiguousarray(cw.reshape(3, 8, 128).transpose(2, 1, 0)).reshape(128, 24)
    lp = inp["lb_param"]
    sh["lbp"] = np.ascontiguousarray(lp.reshape(2, 8, 128).transpose(2, 1, 0)).reshape(128, 16)
    sh["gn"] = np.ascontiguousarray(inp["gnorm_g"][0].reshape(128, 1))
    sh["w_conv"] = _tile_cols(w_in, [np.concatenate([ar(j * 128, (j + 1) * 128), 1024 + ar(j * 128, (j + 1) * 128),
                                                     2048 + ar(j * 128, (j + 1) * 128)]) for j in range(8)])
    sh["w_hq"] = _tile_cols(w_in, [np.concatenate([3072 + ar(j * 128, (j + 1) * 128), 4096 + ar(j * 128, (j + 1) * 128),
                                                   6144 + ar(j * 128, (j + 1) * 128)]) for j in range(8)])
    sh["w_f"] = _tile_cols(w_in, [4096 + ar(j * 128, (j + 1) * 128) for j in range(8)])
    wv = w_in[:, 5120:6144]
    sh["w_v"] = np.stack([_tile_cols(wv[kh * 1024:(kh + 1) * 1024], [ar(cb * 512, (cb + 1) * 512)])[0]
                          for cb in range(2) for kh in range(2)])
    wg = _tile_cols(w_in, [np.concatenate([7168 + ar(j * 128, (j + 1) * 128), 9216 + ar(j * 128, (j + 1) * 128)])
                           for j in range(16)])
    wyo = np.concatenate([inp["w_conv_out"][0], inp["w_hgrn_out"][0]], axis=0)
    wy = _tile_cols(wyo, [ar(j * 128, (j + 1) * 128) for j in range(16)])
    sh["w_gm"] = np.ascontiguousarray(np.concatenate([wg, wy], axis=2))
    sh["w_o"] = _tile_cols(inp["w_o"][0], [ar(j * 256, (j + 1) * 256) for j in range(8)])
    wgate, wup = inp["w_ffn_gate"][0], inp["w_ffn_up"][0]
    wgu = np.concatenate([wgate.reshape(D, 44, 128), wup.reshape(D, 44, 128)], axis=2).reshape(D, 44 * 256)
    sh["w_gu"] = _tile_cols(wgu, [ar(j * 256, (j + 1) * 256) for j in range(44)])
    wd = inp["w_ffn_down"][0]
    sh["w_dn"] = np.stack([_tile_cols(wd[qq * 1408:(qq + 1) * 1408], [ar(cb * 512, (cb + 1) * 512)])[0]
                           for qq in range(4) for cb in range(4)])
    return sh


def _in_maps(inp):
    sh = _prep_shared({k: np.asarray(v, dtype=np.float32) for k, v in inp.items()})
    x = np.asarray(inp["x"], dtype=np.float32)
    c = np.asarray(inp["c"], dtype=np.float32)
    bada = _col(np.asarray(inp["b_ada"], dtype=np.float32)[0])
    zeros = np.zeros((2048, 2048), np.float32)
    maps = []
    for core in range(8):
        b, half = core // 2, core % 2
        m = dict(sh)
        m["xo"] = np.ascontiguousarray(x[b, half * 2048:(half + 1) * 2048])
        m["xp"] = np.ascontiguousarray(x[b, 0:2048]) if half == 1 else zeros
        m["flag"] = np.full((128, 1), float(half), np.float32)
        m["ccol"] = _col(c[b])
        m["badacol"] = bada
        maps.append(m)
    return maps


def kernel(**inputs):
    maps = _in_maps(inputs)
    if "nc" not in _CACHE:
        _CACHE["nc"] = build_nc()
    res = run_bass_kernel_spmd(_CACHE["nc"], maps, core_ids=list(range(8)))
    out = np.empty((4, 4096, 2048), np.float32)
    for core in range(8):
        b, half = core // 2, core % 2
        out[b, half * 2048:(half + 1) * 2048] = res.results[core]["out"]
    return out
```

```python
import os
import numpy as np
from contextlib import ExitStack
import concourse.bass as bass
import concourse.mybir as mybir
from concourse.bass_utils import run_bass_kernel_spmd

F32 = mybir.dt.float32
BF16 = mybir.dt.bfloat16
AF = mybir.ActivationFunctionType
ALU = mybir.AluOpType

ENGS = ["pe", "act", "dve", "pool", "sp"]
SAME_ENGINE_SYNC = True
EPS = 1e-6
BIG = 3.0e38
DEBUG = os.environ.get("MK_DEBUG", "")


class StopProgram(Exception):
    pass


class Sched:
    def __init__(self):
        self.plan = False
        self.reset()

    def reset(self):
        self.streams = {e: [] for e in ENGS}
        self.cnt = {}
        self.seen = {e: {} for e in ENGS}
        self.lastw = {}
        self.readers = {}
        self.sem_names = ["c_" + e for e in ENGS if e != "sp"]

    def op(self, eng, fn, reads=(), writes=(), dsem=None):
        if self.plan:
            return None
        writes = list(writes) + [k for k in reads if k.startswith("ps") and k not in writes]
        need = {}

        def add(tok):
            if tok is None:
                return
            s, v = tok
            if need.get(s, 0) < v:
                need[s] = v

        for k in list(reads) + list(writes):
            add(self.lastw.get(k))
        for k in writes:
            for t in self.readers.get(k, ()):
                add(t)
        own = "c_" + eng
        waits = []
        for s, v in need.items():
            if s == own and (eng == "pe" or not SAME_ENGINE_SYNC):
                continue
            if self.seen[eng].get(s, 0) < v:
                waits.append((s, v))
                self.seen[eng][s] = v
        if dsem is not None:
            if dsem not in self.sem_names:
                self.sem_names.append(dsem)
            self.cnt[dsem] = self.cnt.get(dsem, 0) + 16
            tok = (dsem, self.cnt[dsem])
            inc = 16
        else:
            self.cnt[own] = self.cnt.get(own, 0) + 1
            tok = (own, self.cnt[own])
            inc = 1
        self.streams[eng].append((waits, fn, tok, inc))
        for k in reads:
            self.readers.setdefault(k, []).append(tok)
        for k in writes:
            self.lastw[k] = tok
            self.readers[k] = []
        return tok

    def wait_all(self, eng, keys):
        need = {}
        for k in keys:
            t = self.lastw.get(k)
            if t is not None and need.get(t[0], 0) < t[1]:
                need[t[0]] = t[1]
        self.streams[eng].append((list(need.items()), None, None, 0))

    def replay(self, block, sems):
        engobj = {"pe": "tensor", "act": "scalar", "dve": "vector", "pool": "gpsimd", "sp": "sync"}

        def run(ename):
            def body(eng):
                for waits, fn, tok, inc in self.streams[ename]:
                    for s, v in waits:
                        eng.wait_ge(sems[s], v)
                    if fn is None:
                        continue
                    inst = fn(eng)
                    inst.then_inc(sems[tok[0]], inc)
            return body

        for e in ENGS:
            if self.streams[e]:
                getattr(block, engobj[e])(run(e))


def build_nc(debug=""):
    nc = bass.Bass("TRN2", target_bir_lowering=False, dynamic_dma_scratch_size=4096)

    def din(name, shape):
        return nc.dram_tensor(name, shape, F32, kind="ExternalInput").ap()

    xo = din("xo", [2048, 2048])
    xp = din("xp", [2048, 2048])
    flag_d = din("flag", [128, 1])
    ccol_d = din("ccol", [128, 16])
    wada_d = din("wada", [32, 128, 6144])
    bada_d = din("badacol", [128, 96])
    gmix_d = din("gmix", [128, 16])
    gffn_d = din("gffn", [128, 16])
    gfin_d = din("gfin", [128, 16])
    convw_d = din("convw", [128, 24])
    lbp_d = din("lbp", [128, 16])
    gn_d = din("gn", [128, 1])
    wconv_d = din("w_conv", [8, 128, 6144])
    whq_d = din("w_hq", [8, 128, 6144])
    wf_d = din("w_f", [8, 128, 2048])
    wv_d = din("w_v", [4, 128, 4096])
    wgm_d = din("w_gm", [16, 128, 6144])
    wo_d = din("w_o", [8, 128, 4096])
    wgu_d = din("w_gu", [44, 128, 4096])
    wdn_d = din("w_dn", [16, 128, 5632])
    out_d = nc.dram_tensor("out", [2048, 2048], F32, kind="ExternalOutput").ap()
    dbg_d = None
    if debug:
        dbg_d = nc.dram_tensor("dbg", [128, 16384], F32, kind="ExternalOutput").ap()

    S = Sched()
    with ExitStack() as es:
        def sb(name, shape, dt):
            return es.enter_context(nc.sbuf_tensor(name, shape, dt))

        RX = sb("RX", [128, 16, 1024], F32)
        RH = sb("RH", [128, 16, 1024], BF16)
        RM = sb("RM", [128, 16, 1024], BF16)
        RU = sb("RU", [128, 16, 1024], BF16)
        WR = [sb(f"WR{i}", [128, 6144], BF16) for i in range(3)]
        HG = sb("HG", [128, 2560], BF16)
        ones_f = sb("ones_f", [128, 128], F32)
        ident_f = sb("ident_f", [128, 128], F32)
        ident_b = sb("ident_b", [128, 128], BF16)
        ones_b = sb("ones_b", [128, 128], BF16)
        smask = sb("smask", [128, 512], BF16)
        pmask = sb("pmask", [128, 4, 128], BF16)
        Sm = sb("Sm", [128, 8, 128], F32)
        Sp = sb("Sp", [128, 8, 128], BF16)
        Sh = sb("Sh", [128, 7, 128], F32)
        uhalo = sb("uhalo", [128, 8, 2], F32)
        tl2 = sb("tl2", [128, 2], F32)
        httail = sb("httail", [128, 16, 2], BF16)
        modcol = sb("modcol", [128, 96], F32)
        gmodm = sb("gmodm", [128, 16], F32)
        gmodf = sb("gmodf", [128, 16], F32)
        badac = sb("badac", [128, 96], F32)
        ccol = sb("ccol_s", [128, 16], F32)
        cact = sb("cact", [128, 16], BF16)
        gmix = sb("gmix_s", [128, 16], F32)
        gffn = sb("gffn_s", [128, 16], F32)
        gfin = sb("gfin_s", [128, 16], F32)
        convw = sb("convw_s", [128, 8, 3], F32)
        lbp = sb("lbp_s", [128, 8, 2], F32)
        gn = sb("gn_s", [128, 1], F32)
        flag = sb("flag_s", [128, 1], F32)
        lb = sb("lb", [128, 8], F32)
        oml = sb("oml", [128, 8], F32)
        noml = sb("noml", [128, 8], F32)
        em = sb("em", [128, 8], F32)
        dd = sb("dd", [128, 8], F32)
        epsc = sb("epsc", [128, 1], F32)
        ps = [es.enter_context(nc.psum_tensor(f"ps{i}", [128, 512], F32)) for i in range(8)]

        RMf = RM[:].rearrange("p k t -> p (k t)").bitcast(F32)
        RUf = RU[:].rearrange("p k t -> p (k t)").bitcast(F32)
        HGf = HG[:, 0:2048].bitcast(F32)

        def scr(i, n=512):
            return RMf[:, i * 512:i * 512 + n]

        def scrb(i):
            return RM[:, i, 0:512]

        Qp = HG[:, 0:512]
        Kp = HG[:, 512:1024]
        Ktok = HG[:, 1024:1536].rearrange("p (q k) -> p q k", q=4)
        Pm = HG[:, 1536:2048]
        sqo = HG[:, 2048:2560]

        def hs(h):
            return slice(h * 512, (h + 1) * 512)

        def RXk(kcs, h):
            return [f"RX{k}t{t}" for k in kcs for t in range(4 * h, 4 * h + 4)]

        def RHk(h):
            return [f"RH{k}h{h}" for k in range(16)]

        op = S.op

        def dstop(name):
            if debug == name:
                raise StopProgram()

        def ACT(out, in_, func, reads, writes, **kw):
            op("act", lambda e: e.activation(out=out, in_=in_, func=func, **kw), reads=reads, writes=writes)

        def TT(out, in0, in1, alu, reads, writes):
            op("dve", lambda e: e.tensor_tensor(out=out, in0=in0, in1=in1, op=alu), reads=reads, writes=writes)

        def PTT(out, in0, in1, alu, reads, writes):
            op("pool", lambda e: e.tensor_tensor(out=out, in0=in0, in1=in1, op=alu), reads=reads, writes=writes)

        def TS(out, in0, s1, s2, op0, op1, reads, writes):
            if s2 is None:
                op("dve", lambda e: e.tensor_scalar(out=out, in0=in0, scalar1=s1, scalar2=None, op0=op0),
                   reads=reads, writes=writes)
            else:
                op("dve", lambda e: e.tensor_scalar(out=out, in0=in0, scalar1=s1, scalar2=s2, op0=op0, op1=op1),
                   reads=reads, writes=writes)

        def STT(out, in0, scalar, in1, op0, op1, reads, writes):
            op("dve", lambda e: e.scalar_tensor_tensor(out=out, in0=in0, scalar=scalar, in1=in1, op0=op0, op1=op1),
               reads=reads, writes=writes)

        def MM(mms, reads, writes):
            mms = list(mms)

            def f(e):
                last = None
                for o, l, r, st, sp_ in mms:
                    last = e.matmul(o, lhsT=l, rhs=r, start=st, stop=sp_)
                return last
            op("pe", f, reads=reads, writes=writes)

        def TR(trs, reads, writes):
            trs = list(trs)

            def f(e):
                last = None
                for o, i, idn in trs:
                    last = e.transpose(out=o, in_=i, identity=idn)
                return last
            op("pe", f, reads=reads, writes=writes)

        wlist = []
        wst = {"i": 0, "iss": 0}

        def wload(s, src_k, n_k):
            op("pool", lambda e: e.dma_start(out=WR[s][:, 0:n_k], in_=src_k), writes=[f"W{s}"], dsem=f"d_w{s}")

        released = set()
        manual = set()

        def pump():
            lim = min(len(wlist), wst["i"] + 2)
            while wst["iss"] < lim:
                k = wst["iss"]
                if k >= 3 and (k - 3) not in released:
                    break
                if wlist[k][0] is not None:
                    wload(k % 3, wlist[k][0], wlist[k][1])
                wst["iss"] += 1

        def wtake(src, n, hold=0, man=False):
            if S.plan:
                wlist.append((src, n))
                wst["last"] = len(wlist) - 1
                return 0
            i = wst["i"]
            wst["i"] += 1
            wst["last"] = i
            if man:
                manual.add(i)
            for k in range(max(0, i - 4), i - hold):
                if k not in manual:
                    released.add(k)
            pump()
            assert wst["iss"] > i, ("weight tile not loadable (ring slot still held)", i)
            return i % 3

        def wdone(idx):
            if S.plan:
                return
            released.add(idx)
            pump()

        alt = {"i": 0}

        def COPY(out, in_, reads, writes, eng=None):
            if eng is None:
                alt["i"] += 1
                eng = "act" if alt["i"] % 2 else "dve"
            if eng == "act":
                ACT(out, in_, AF.Copy, reads, writes)
            else:
                op("dve", lambda e: e.tensor_copy(out=out, in_=in_), reads=reads, writes=writes)

        def DMA(eng, out, in_, reads, writes, dsem):
            op(eng, lambda e: e.dma_start(out=out, in_=in_), reads=reads, writes=writes, dsem=dsem)

        def W3(s, n, k):
            return WR[s][:, 0:n].rearrange("p (k c) -> p k c", k=k)

        def setup():
            smalls = [(flag[:], flag_d, "flag"), (ccol[:], ccol_d, "ccol"), (badac[:], bada_d, "badac"),
                      (gmix[:], gmix_d, "gmix"), (gffn[:], gffn_d, "gffn"), (gfin[:], gfin_d, "gfin"),
                      (convw[:].rearrange("p a b -> p (a b)"), convw_d, "convw"),
                      (lbp[:].rearrange("p a b -> p (a b)"), lbp_d, "lbp"), (gn[:], gn_d, "gn")]
            for t, d, k in smalls:
                DMA("sp", t, d, [], [k], "d_c")
            if not S.plan:
                last = S.lastw[smalls[-1][2]]
                for t, d, k in smalls:
                    S.lastw[k] = last

            def P(fn, reads, writes):
                op("pool", fn, reads=reads, writes=writes)
            P(lambda e: e.memset(ones_f[:], 1.0), [], ["ones_f"])
            P(lambda e: e.memset(ones_b[:], 1.0), [], ["ones_b"])
            P(lambda e: e.memset(epsc[:], EPS), [], ["epsc"])
            P(lambda e: e.memset(Sm[:], 0.0), [], [f"Sm{h}" for h in range(8)])
            P(lambda e: e.memset(uhalo[:], 0.0), [], ["uhalo"])
            P(lambda e: e.affine_select(out=ident_f[:], in_=ones_f[:], pattern=[[-1, 128]], compare_op=ALU.is_equal,
                                        fill=0.0, base=0, channel_multiplier=1), ["ones_f"], ["ident_f"])
            P(lambda e: e.tensor_copy(out=ident_b[:], in_=ident_f[:]), ["ident_f"], ["ident_b"])
            ones3 = ones_f[:, 0:64].unsqueeze(1).to_broadcast([128, 8, 64])
            P(lambda e: e.affine_select(out=smask[:].rearrange("p (c t) -> p c t", c=8), in_=ones3,
                                        pattern=[[0, 8], [1, 64]], compare_op=ALU.not_equal, fill=0.0, base=0,
                                        channel_multiplier=0), ["ones_f"], ["smask"])
            onesb = ones_f[:].unsqueeze(1).to_broadcast([128, 4, 128])
            P(lambda e: e.affine_select(out=pmask[:], in_=onesb, pattern=[[0, 4], [1, 128]], compare_op=ALU.is_ge,
                                        fill=0.0, base=0, channel_multiplier=-1), ["ones_f"], ["pmask"])
            P(lambda e: e.affine_select(out=pmask[:, :, 64:128], in_=pmask[:, :, 64:128], pattern=[[0, 4], [0, 64]],
                                        compare_op=ALU.is_ge, fill=0.0, base=-64, channel_multiplier=1),
              ["pmask"], ["pmask"])
            TT(lb[:], lbp[:, :, 0], lbp[:, :, 1], ALU.subtract, ["lbp"], ["lb"])
            ACT(lb[:], lb[:], AF.Sigmoid, ["lb"], ["lb"])
            TS(oml[:], lb[:], -1.0, 1.0, ALU.mult, ALU.add, ["lb"], ["oml"])
            TS(noml[:], lb[:], -1.0, None, ALU.add, None, ["lb"], ["noml"])
            ACT(cact[:], ccol[:], AF.Silu, ["ccol"], ["cact"])

        mod_pending = []

        def mod_some(n, banks=(0, 2)):
            k = min(n, len(mod_pending))
            if k:
                tiles = mod_pending[:k]
                del mod_pending[:k]
                mod(tiles, banks=banks)

        def mod(tiles, banks=(0, 1)):
            for t in tiles:
                s = wtake(wada_d[t], 6144)
                Wt = W3(s, 6144, 16)
                b = banks[t % 2]
                MM([(ps[b][:, g:g + 1], Wt[:, kc, g * 128:(g + 1) * 128], cact[:, kc:kc + 1], kc == 0, kc == 15)
                    for g in range(3) for kc in range(16)], [f"W{s}", "cact"], [f"ps{b}"])
                TT(modcol[:, 3 * t:3 * t + 3], ps[b][:, 0:3], badac[:, 3 * t:3 * t + 3], ALU.add,
                   [f"ps{b}", "badac"], ["modcol"])

        def mk_gmod(dst, g, key_g, c0, key):
            STT(dst[:], modcol[:, c0:c0 + 16], 1.0, g[:], ALU.add, ALU.mult, ["modcol", key_g], [key])

        def loadx(src, sbi, nmod=0):
            for tt in range(8):
                if nmod:
                    mod_some(nmod, banks=(6, 7))
                st = tt % 2
                xst = RUf[:, st * 2048:(st + 1) * 2048]
                skeys = [f"RU{4 * st + q}" for q in range(4)]
                r0 = sbi * 1024 + tt * 128
                DMA("sp", xst, src[r0:r0 + 128, :], [], skeys, f"d_x{st}")
                for g in range(4):
                    b = (tt * 4 + g) % 4
                    TR([(ps[b][:, q * 128:(q + 1) * 128], xst[:, (4 * g + q) * 128:(4 * g + q + 1) * 128], ident_f[:])
                        for q in range(4)], skeys + ["ident_f"], [f"ps{b}"])
                    COPY(RX[:, 4 * g:4 * g + 4, tt * 128:(tt + 1) * 128], ps[b][:].rearrange("p (q t) -> p q t", q=4),
                         [f"ps{b}"], [f"RX{4 * g + q}t{tt}" for q in range(4)])

        def rstd_half(h, dst, dkey, pb, sq_slots):
            for kc in range(16):
                sl = sq_slots[kc % 2]
                xin = RX[:, kc, hs(h)]
                if kc % 4 == 3:
                    ACT(scrb(sl), xin, AF.Square, RXk([kc], h), [f"RM{sl}"])
                else:
                    PTT(scrb(sl), xin, xin, ALU.mult, RXk([kc], h), [f"RM{sl}"])
                MM([(ps[pb][:], ones_b[:], scrb(sl), kc == 0, kc == 15)], [f"RM{sl}", "ones_b"], [f"ps{pb}"])
            ACT(dst, ps[pb][:], AF.Ln, [f"ps{pb}", "epsc"], [dkey], scale=1.0 / 2048.0, bias=epsc[:])
            ACT(dst, dst, AF.Exp, [dkey], [dkey], scale=-0.5)

        def norm_to_RH(gmod, sh_c0, gkey):
            for h in range(2):
                rs = scr(10 + h)
                rstd_half(h, rs, f"RM{10 + h}", 4 + h, (12, 13))
                for kc in range(16):
                    ts = 14 + kc % 2
                    TT(scr(ts), RX[:, kc, hs(h)], rs, ALU.mult, RXk([kc], h) + [f"RM{10 + h}"], [f"RM{ts}"])
                    ACT(RH[:, kc, hs(h)], scr(ts), AF.Identity, [f"RM{ts}", gkey, "modcol"], [f"RH{kc}h{h}"],
                        scale=gmod[:, kc:kc + 1], bias=modcol[:, sh_c0 + kc:sh_c0 + kc + 1])

        RXf = RX[:].rearrange("p k t -> p (k t)")
        conv_state = {}

        def conv_item(k, first=False):
            j, h = k // 2, k % 2
            so = 4 * (k % 2)
            def sl(i, n=512):
                return RXf[:, (so + i) * 1024:(so + i) * 1024 + n]
            def sk(i):
                return RXk([so + i], 0) + RXk([so + i], 1)
            t1, ub, y1, y2 = sl(0), sl(1, 514), sl(2), sl(3)
            T1, UB, Y1, Y2 = sk(0), sk(1), sk(2), sk(3)
            if h == 0:
                conv_state["s"] = wtake(wconv_d[j], 6144, man=True)
                conv_state["idx"] = wst["last"]
            s = conv_state["s"]
            Wt = W3(s, 6144, 16)
            if first and h == 0:
                MM([(ps[3][:, 2 * g:2 * g + 2], Wt[:, kc, (g + 1) * 128:(g + 2) * 128], httail[:, kc, :], kc == 0,
                     kc == 15) for g in range(2) for kc in range(16)], [f"W{s}", "httail"], ["ps3"])
                ACT(tl2[:], ps[3][:, 0:2], AF.Copy, ["ps3"], ["tl2"])
                STT(uhalo[:, j, :], ps[3][:, 2:4], flag[:, 0:1], tl2[:], ALU.mult, ALU.mult, ["ps3", "flag", "tl2"],
                    ["uhalo"])
            pb = 3
            MM([(ps[pb + g][:], Wt[:, kc, g * 128:(g + 1) * 128], RH[:, kc, hs(h)], kc == 0, kc == 15)
                for g in (1, 2) for kc in range(16)], [f"W{s}"] + RHk(h), [f"ps{pb + 1}", f"ps{pb + 2}"])
            ACT(t1, ps[pb + 1][:], AF.Copy, [f"ps{pb + 1}"], T1)
            ACT(ub[:, 0:2], uhalo[:, j, :], AF.Copy, ["uhalo"], UB)
            TT(ub[:, 2:514], ps[pb + 2][:], t1, ALU.mult, [f"ps{pb + 2}"] + T1, UB)
            ACT(uhalo[:, j, :], ub[:, 512:514], AF.Copy, UB, ["uhalo"])
            TS(y1, ub[:, 0:512], convw[:, j, 0:1], None, ALU.mult, None, UB + ["convw"], Y1)
            STT(y2, ub[:, 1:513], convw[:, j, 1:2], y1, ALU.mult, ALU.add, UB + ["convw"] + Y1, Y2)
            STT(y1, ub[:, 2:514], convw[:, j, 2:3], y2, ALU.mult, ALU.add, UB + ["convw"] + Y2, Y1)
            conv_state["fin"] = (Wt, s, j, h, y1, Y1)

        def conv_item_b_mm():
            Wt, s, j, h, y1, Y1 = conv_state["fin"]
            MM([(ps[3][:], Wt[:, kc, 0:128], RH[:, kc, hs(h)], kc == 0, kc == 15) for kc in range(16)],
               [f"W{s}"] + RHk(h), ["ps3"])
            if h == 1:
                wdone(conv_state["idx"])

        def conv_item_b_tt():
            Wt, s, j, h, y1, Y1 = conv_state["fin"]
            TT(RU[:, j, hs(h)], ps[3][:], y1, ALU.mult, ["ps3"] + Y1, [f"RU{j}"])

        def reloadx_start(src, sbi, tt):
            s = wtake(None, 4096, man=True)
            idx = wst["last"]
            stage = WR[s][:, 0:4096].bitcast(F32)
            r0 = sbi * 1024 + tt * 128
            DMA("sp", stage, src[r0:r0 + 128, :], [], [f"W{s}"], f"d_w{s}")
            return (s, idx, stage, tt)

        def reloadx_finish(st, bank0=4):
            s, idx, stage, tt = st
            for g in range(4):
                b = bank0 + g
                TR([(ps[b][:, q * 128:(q + 1) * 128], stage[:, (4 * g + q) * 128:(4 * g + q + 1) * 128], ident_f[:])
                    for q in range(4)], [f"W{s}", "ident_f"], [f"ps{b}"])
                COPY(RX[:, 4 * g:4 * g + 4, tt * 128:(tt + 1) * 128], ps[b][:].rearrange("p (q t) -> p q t", q=4),
                     [f"ps{b}"], [f"RX{4 * g + q}t{tt}" for q in range(4)])
            wdone(idx)

        def conv_tail():
            for j in range(8):
                s = wtake(wconv_d[j], 6144)
                Wt = W3(s, 6144, 16)
                b = 6 + j % 2
                MM([(ps[b][:, 2 * g:2 * g + 2], Wt[:, kc, (g + 1) * 128:(g + 2) * 128], RH[:, kc, 1022:1024], kc == 0,
                     kc == 15) for g in range(2) for kc in range(16)], [f"W{s}"] + RHk(1), [f"ps{b}"])
                ACT(tl2[:], ps[b][:, 0:2], AF.Copy, [f"ps{b}"], ["tl2"])
                STT(uhalo[:, j, :], ps[b][:, 2:4], flag[:, 0:1], tl2[:], ALU.mult, ALU.mult, [f"ps{b}", "flag", "tl2"],
                    ["uhalo"])

        def vproj(prefix):
            for cb in range(2):
                s0 = wtake(wv_d[cb * 2], 4096)
                s1 = wtake(wv_d[cb * 2 + 1], 4096, hold=1)
                Ws = [W3(s0, 4096, 8), W3(s1, 4096, 8)]
                for tt in range(8):
                    b = 3 + (cb * 8 + tt) % 4
                    MM([(ps[b][:], RH[:, kc, tt * 128:(tt + 1) * 128], Ws[kc // 8][:, kc % 8, :], kc == 0, kc == 15)
                        for kc in range(16)], [f"W{s0}", f"W{s1}"] + RHk(tt // 4), [f"ps{b}"])
                    dst = RM[:, tt, cb * 512:(cb + 1) * 512]
                    if prefix:
                        ACT(dst, ps[b][:], AF.Copy, [f"ps{b}", "flag"], [f"RM{tt}"], scale=flag[:, 0:1])
                    else:
                        COPY(dst, ps[b][:], [f"ps{b}"], [f"RM{tt}"])

        def hgrn(prefix, first=False):
            items = [(hd, h) for hd in range(8) for h in range(2)]
            n_it = len(items)
            wslot = {}
            widx = {}
            L = 2 if prefix else 1
            t = [scr(8 + k) for k in range(8)]
            K = [f"RM{8 + k}" for k in range(8)]
            X, Y = 6, 7
            psXb = ps[X][:].bitcast(BF16)
            bc3 = t[2].rearrange("p (c t) -> p c t", c=8)
            bm3 = t[6].rearrange("p (c t) -> p c t", c=8)

            def bigpart(i, part):
                if i >= n_it:
                    return
                hd, h = items[i]
                s3 = 3 * (i % 2) if prefix else 0
                if part == 0 and h == 0:
                    if prefix:
                        wslot[hd] = wtake(wf_d[hd], 2048)
                    else:
                        wslot[hd] = wtake(whq_d[hd], 6144, man=True)
                        widx[hd] = wst["last"]
                s = wslot[hd]
                if prefix:
                    if part != 0:
                        return
                    Wt = W3(s, 2048, 16)
                    MM([(ps[s3 + 1][:], Wt[:, kc, :], RH[:, kc, hs(h)], kc == 0, kc == 15) for kc in range(16)],
                       [f"W{s}"] + RHk(h), [f"ps{s3 + 1}"])
                else:
                    Wt = W3(s, 6144, 16)
                    g = part
                    MM([(ps[s3 + g][:], Wt[:, kc, g * 128:(g + 1) * 128], RH[:, kc, hs(h)], kc == 0, kc == 15)
                        for kc in range(16)], [f"W{s}"] + RHk(h), [f"ps{s3 + g}"])
                    if part == 2 and h == 1:
                        wdone(widx[hd])

            def elem(i):
                hd, h = items[i]
                s3 = 3 * (i % 2) if prefix else 0
                pq, pf, pg = ps[s3], ps[s3 + 1], ps[s3 + 2]
                ACT(t[0], pf[:], AF.Sigmoid, [f"ps{s3 + 1}"], [K[0]])
                if not prefix:
                    ACT(t[4], pq[:], AF.Silu, [f"ps{s3}"], [K[4]])
                    ACT(t[5], pg[:], AF.Silu, [f"ps{s3 + 2}"], [K[5]])
                ACT(t[1], t[0], AF.Ln, [K[0], "oml", "lb"], [K[1]], scale=oml[:, hd:hd + 1], bias=lb[:, hd:hd + 1])
                op("dve", lambda e: e.tensor_tensor_scan(out=t[2], data0=smask[:], data1=t[1], initial=0.0,
                                                         op0=ALU.mult, op1=ALU.add),
                   reads=[K[1], "smask"], writes=[K[2]])
                TS(t[3], t[0], noml[:, hd:hd + 1], oml[:, hd:hd + 1], ALU.mult, ALU.add, [K[0], "noml", "oml"], [K[3]])
                ACT(dd[:], bc3[:, :, 63], AF.Exp, [K[2]], ["dd"])
                if not prefix:
                    ACT(em[:], bc3[:, :, 32], AF.Exp, [K[2]], ["em"])
                TT(bm3, bc3, bc3[:, :, 32:33].to_broadcast([128, 8, 64]), ALU.subtract, [K[2]], [K[6]])
                ACT(t[1], t[6], AF.Exp, [K[6]], [K[1]], scale=-1.0)
                if prefix:
                    ACT(bm3[:, :, 63], bm3[:, :, 63], AF.Exp, [K[6]], [K[6]])
                else:
                    ACT(t[6], t[6], AF.Exp, [K[6]], [K[6]])
                if not prefix:
                    TT(Qp, t[4], t[6], ALU.mult, [K[4], K[6]], ["Qp"])
                TT(Kp, t[3], t[1], ALU.mult, [K[3], K[1]], ["Kp"])

            def small(i):
                hd, h = items[i]
                TR([(psXb[:, q * 128:(q + 1) * 128], Kp[:, q * 128:(q + 1) * 128], ident_b[:]) for q in range(4)],
                   ["Kp", "ident_b"], [f"ps{X}"])
                ACT(Ktok, psXb[:, 0:512].rearrange("p (q k) -> p q k", q=4), AF.Copy, [f"ps{X}"], ["Ktok"])
                if not prefix:
                    MM([(ps[Y][:, q * 128:(q + 1) * 128], Kp[:, q * 128:(q + 1) * 128], Qp[:, q * 128:(q + 1) * 128],
                         True, True) for q in range(4)], ["Kp", "Qp"], [f"ps{Y}"])
                    STT(Pm, ps[Y][:], BIG, pmask[:].rearrange("p a b -> p (a b)"), ALU.min, ALU.mult,
                        [f"ps{Y}", "pmask"], ["Pm"])
                    dstop("s1")
                bigpart(i + L, 0)
                vt = [f"RM{4 * h + q}" for q in range(4)]
                AB = [X, Y]
                MM([(ps[AB[c % 2]][:, (c // 2) * 128:(c // 2 + 1) * 128],
                     Ktok[(c % 2) * 64:(c % 2) * 64 + 64, c // 2, :],
                     RM[(c % 2) * 64:(c % 2) * 64 + 64, 4 * h + c // 2, hd * 128:(hd + 1) * 128], True, True)
                    for c in range(8)], ["Ktok"] + vt, [f"ps{X}", f"ps{Y}"])
                bigpart(i + L, 1)
                if not prefix:
                    conv_item_b_mm()
                t6q = t[6].rearrange("p (q two t) -> p q two t", two=2, t=64)
                for par in range(2):
                    pv = ps[AB[par]][:].rearrange("p (q v) -> p q v", q=4)
                    TT(pv, pv, t6q[:, :, par, 63:64].to_broadcast([128, 4, 128]), ALU.mult,
                       [f"ps{AB[par]}", K[6]], [f"ps{AB[par]}"])
                for c in range(8):
                    srcS = Sm[:, hd, :] if c == 0 else Sh[:, c - 1, :]
                    skey = f"Sm{hd}" if c == 0 else f"Sh{c - 1}"
                    dstS = Sm[:, hd, :] if c == 7 else Sh[:, c, :]
                    dkey = f"Sm{hd}" if c == 7 else f"Sh{c}"
                    if not prefix:
                        ACT(Sp[:, c, :], srcS, AF.Copy, [skey, "em"], [f"Sp{c}"], scale=em[:, c:c + 1])
                    STT(dstS, srcS, dd[:, c:c + 1], ps[AB[c % 2]][:, (c // 2) * 128:(c // 2 + 1) * 128],
                        ALU.mult, ALU.add, [skey, "dd", f"ps{AB[c % 2]}"], [dkey])
                if prefix:
                    bigpart(i + L, 2)
                    return
                dstop("s2")
                mms = []
                for q in range(4):
                    mms.append((ps[X][:, q * 128:(q + 1) * 128], RM[:, 4 * h + q, hd * 128:(hd + 1) * 128],
                                Pm[:, q * 128:(q + 1) * 128], True, False))
                    for cc in range(2):
                        c = 2 * q + cc
                        mms.append((ps[X][:, c * 64:(c + 1) * 64], Sp[:, c, :], Qp[:, c * 64:(c + 1) * 64], False,
                                    cc == 1))
                MM(mms, vt + ["Pm", "Qp"] + [f"Sp{c}" for c in range(8)], [f"ps{X}"])
                bigpart(i + L, 2)
                conv_item_b_tt()
                dstop("s3")
                ACT(sqo, ps[X][:], AF.Square, [f"ps{X}"], ["sqo"])
                TT(t[7], ps[X][:], t[5], ALU.mult, [f"ps{X}", K[5]], [K[7]])
                MM([(ps[Y][:], ones_b[:], sqo, True, True)], ["sqo", "ones_b"], [f"ps{Y}"])
                ACT(t[0], ps[Y][:], AF.Ln, [f"ps{Y}", "epsc"], [K[0]], scale=1.0 / 128.0, bias=epsc[:])
                ACT(t[0], t[0], AF.Exp, [K[0]], [K[0]], scale=-0.5)
                STT(RU[:, 8 + hd, hs(h)], t[7], gn[:, 0:1], t[0], ALU.mult, ALU.mult, [K[7], "gn", K[0]],
                    [f"RU{8 + hd}"])

            for i0 in range(L):
                for part in range(3):
                    bigpart(i0, part)
            for i in range(n_it):
                elem(i)
                if prefix and items[i][1] == 0:
                    mod_some(prefix)
                if not prefix:
                    conv_item(i, first)
                small(i)
                if not prefix:
                    dstop("s4")

        def hgrn_prefix(nmod):
            def rxs(n, w=512):
                return RXf[:, n * 1024:n * 1024 + w], RXk([n], 0) + RXk([n], 1)
            pipes = []
            for p in range(2):
                if p == 0:
                    tl = [(scr(8 + k), [f"RM{8 + k}"]) for k in range(5)]
                    bs = dict(t=tl, Kp=(Kp, ["Kp"]), Ktok=(Ktok, ["Ktok"]), dd=(dd, ["dd"]),
                              Sh=(Sh, [f"Sh{c}" for c in range(7)]), F=1, X=6, Y=7)
                else:
                    tl = [rxs(k) for k in range(5)]
                    shv, shk = rxs(5, 896)
                    bs = dict(t=tl, Kp=(Qp, ["Qp"]), Ktok=(HG[:, 1536:2048].rearrange("p (q k) -> p q k", q=4), ["Pm"]),
                              dd=(em, ["em"]), Sh=(shv.rearrange("p (c v) -> p c v", c=7), shk), F=4, X=3, Y=5)
                bs["heads"] = [0, 2, 4, 6] if p == 0 else [1, 3, 5, 7]
                pipes.append(bs)

            def proj(bs, hd, h):
                if h == 0:
                    bs["s"] = wtake(wf_d[hd], 2048, man=True)
                    bs["idx"] = wst["last"]
                s = bs["s"]
                Wt = W3(s, 2048, 16)
                F = bs["F"]
                MM([(ps[F][:], Wt[:, kc, :], RH[:, kc, hs(h)], kc == 0, kc == 15) for kc in range(16)],
                   [f"W{s}"] + RHk(h), [f"ps{F}"])
                if h == 1:
                    wdone(bs["idx"])

            def st1(bs, hd, h):
                (sg, ksg), (lf, klf), (bc, kbc), (kk, kkk), (bm, kbm) = bs["t"]
                F = bs["F"]
                ACT(sg, ps[F][:], AF.Sigmoid, [f"ps{F}"], ksg)
                ACT(lf, sg, AF.Ln, ksg + ["oml", "lb"], klf, scale=oml[:, hd:hd + 1], bias=lb[:, hd:hd + 1])
                op("dve", lambda e: e.tensor_tensor_scan(out=bc, data0=smask[:], data1=lf, initial=0.0,
                                                         op0=ALU.mult, op1=ALU.add),
                   reads=klf + ["smask"], writes=kbc)
                TS(kk, sg, noml[:, hd:hd + 1], oml[:, hd:hd + 1], ALU.mult, ALU.add, ksg + ["noml", "oml"], kkk)
                bc3 = bc.rearrange("p (c t) -> p c t", c=8)
                ddv, kdd = bs["dd"]
                ACT(ddv[:], bc3[:, :, 63], AF.Exp, kbc, kdd)

            def st2(bs, hd, h):
                (sg, ksg), (lf, klf), (bc, kbc), (kk, kkk), (bm, kbm) = bs["t"]
                bc3 = bc.rearrange("p (c t) -> p c t", c=8)
                bm3 = bm.rearrange("p (c t) -> p c t", c=8)
                TT(bm3, bc3, bc3[:, :, 32:33].to_broadcast([128, 8, 64]), ALU.subtract, kbc, kbm)
                ACT(lf, bm, AF.Exp, kbm, klf, scale=-1.0)
                ACT(bm3[:, :, 63], bm3[:, :, 63], AF.Exp, kbm, kbm)
                kpv, kkp = bs["Kp"]
                TT(kpv, kk, lf, ALU.mult, kkk + klf, kkp)

            def st3(bs, hd, h):
                kpv, kkp = bs["Kp"]
                ktv, kkt = bs["Ktok"]
                X = bs["X"]
                psXb_ = ps[X][:].bitcast(BF16)
                TR([(psXb_[:, q * 128:(q + 1) * 128], kpv[:, q * 128:(q + 1) * 128], ident_b[:]) for q in range(4)],
                   kkp + ["ident_b"], [f"ps{X}"])
                ACT(ktv, psXb_[:, 0:512].rearrange("p (q k) -> p q k", q=4), AF.Copy, [f"ps{X}"], kkt)

            def st4(bs, hd, h):
                (sg, ksg), (lf, klf), (bc, kbc), (kk, kkk), (bm, kbm) = bs["t"]
                ktv, kkt = bs["Ktok"]
                AB = [bs["X"], bs["Y"]]
                vt = [f"RM{4 * h + q}" for q in range(4)]
                MM([(ps[AB[c % 2]][:, (c // 2) * 128:(c // 2 + 1) * 128],
                     ktv[(c % 2) * 64:(c % 2) * 64 + 64, c // 2, :],
                     RM[(c % 2) * 64:(c % 2) * 64 + 64, 4 * h + c // 2, hd * 128:(hd + 1) * 128], True, True)
                    for c in range(8)], kkt + vt, [f"ps{AB[0]}", f"ps{AB[1]}"])
                bmq = bm.rearrange("p (q two t) -> p q two t", two=2, t=64)
                for par in range(2):
                    pv = ps[AB[par]][:].rearrange("p (q v) -> p q v", q=4)
                    TT(pv, pv, bmq[:, :, par, 63:64].to_broadcast([128, 4, 128]), ALU.mult,
                       [f"ps{AB[par]}"] + kbm, [f"ps{AB[par]}"])

            def st5(bs, hd, h):
                AB = [bs["X"], bs["Y"]]
                shv, shk = bs["Sh"]
                ddv, kdd = bs["dd"]
                for c in range(8):
                    srcS = Sm[:, hd, :] if c == 0 else shv[:, c - 1, :]
                    skey = [f"Sm{hd}"] if c == 0 else shk
                    dstS = Sm[:, hd, :] if c == 7 else shv[:, c, :]
                    dkey = [f"Sm{hd}"] if c == 7 else shk
                    STT(dstS, srcS, ddv[:, c:c + 1], ps[AB[c % 2]][:, (c // 2) * 128:(c // 2 + 1) * 128],
                        ALU.mult, ALU.add, skey + kdd + [f"ps{AB[c % 2]}"], dkey)

            seq = [[(hd, h) for hd in bs["heads"] for h in range(2)] for bs in pipes]
            for p in range(2):
                proj(pipes[p], *seq[p][0])
            for n in range(8):
                for stage in (st1, st2, st3, st4, st5):
                    for p in range(2):
                        stage(pipes[p], *seq[p][n])
                    if stage is st1 and n + 1 < 8:
                        for p in range(2):
                            proj(pipes[p], *seq[p][n + 1])
                        if n % 2 == 0:
                            mod_some(nmod)

        def gate(sbi):
            pend = []
            sa = HG[:, 1024:1536]
            sbb = HG[:, 1536:2048]
            m1 = HGf[:, 0:512]
            for j in range(16):
                if j % 2 == 0:
                    pend.append(reloadx_start(xo, sbi, j // 2))
                s = wtake(wgm_d[j], 6144, man=True)
                gidx = wst["last"]
                Wg = WR[s][:, 0:4096].rearrange("p (k c) -> p k c", k=16)
                Wy = WR[s][:, 4096:6144].rearrange("p (k c) -> p k c", k=16)
                for h in range(2):
                    pb = 4 * ((j * 2 + h) % 2)
                    mms = [(ps[pb + g][:], Wg[:, kc, g * 128:(g + 1) * 128], RH[:, kc, hs(h)], kc == 0, kc == 15)
                           for g in range(2) for kc in range(16)]
                    mms += [(ps[pb + 2 + g][:], Wy[:, 8 * g + kc, :], RU[:, 8 * g + kc, hs(h)], kc == 0, kc == 7)
                            for g in range(2) for kc in range(8)]
                    MM(mms, [f"W{s}"] + RHk(h) + [f"RU{k}" for k in range(16)], [f"ps{pb + g}" for g in range(4)])
                    if h == 1:
                        wdone(gidx)
                    ACT(sa, ps[pb][:], AF.Sigmoid, [f"ps{pb}"], ["Ktok"])
                    ACT(sbb, ps[pb + 1][:], AF.Sigmoid, [f"ps{pb + 1}"], ["Pm"])
                    TT(m1, ps[pb + 2][:], sa, ALU.mult, [f"ps{pb + 2}", "Ktok"], ["Qp", "Kp"])
                    TT(ps[pb][:], ps[pb + 3][:], sbb, ALU.mult, [f"ps{pb + 3}", "Pm"], [f"ps{pb}"])
                    TT(RM[:, j, hs(h)], ps[pb][:], m1, ALU.add, [f"ps{pb}", "Qp", "Kp"], [f"RM{j}"])
                    if h == 0 and pend:
                        reloadx_finish(pend.pop(0))
            while pend:
                reloadx_finish(pend.pop(0))

        cntr = {"b": 0}

        def wo():
            for jp in range(8):
                s = wtake(wo_d[jp], 4096)
                Wt = W3(s, 4096, 16)
                for jl in range(2):
                    jc = 2 * jp + jl
                    for h in range(2):
                        b = cntr["b"] % 8
                        cntr["b"] += 1
                        MM([(ps[b][:], Wt[:, kc, jl * 128:(jl + 1) * 128], RM[:, kc, hs(h)], kc == 0, kc == 15)
                            for kc in range(16)], [f"W{s}"] + [f"RM{k}" for k in range(16)], [f"ps{b}"])
                        rk = RXk([jc], h)
                        STT(RX[:, jc, hs(h)], ps[b][:], modcol[:, 32 + jc:33 + jc], RX[:, jc, hs(h)], ALU.mult, ALU.add,
                            [f"ps{b}", "modcol"] + rk, rk)

        def ffn():
            sgb = [HGf[:, 0:512], HGf[:, 512:1024]]
            sgk = [["Qp", "Kp"], ["Ktok", "Pm"]]
            cnt = 0
            for qq in range(4):
                for jl in range(11):
                    jj = qq * 11 + jl
                    s = wtake(wgu_d[jj], 4096)
                    Wt = W3(s, 4096, 16)
                    for h in range(2):
                        pb = 2 * (cnt % 4)
                        sg = sgb[cnt % 2]
                        sk = sgk[cnt % 2]
                        cnt += 1
                        MM([(ps[pb + g][:], Wt[:, kc, g * 128:(g + 1) * 128], RH[:, kc, hs(h)], kc == 0, kc == 15)
                            for g in range(2) for kc in range(16)], [f"W{s}"] + RHk(h), [f"ps{pb}", f"ps{pb + 1}"])
                        ACT(sg, ps[pb][:], AF.Silu, [f"ps{pb}"], sk)
                        TT(RM[:, jl, hs(h)], ps[pb + 1][:], sg, ALU.mult, [f"ps{pb + 1}"] + sk, [f"RM{jl}"])
                for cbd in range(4):
                    s = wtake(wdn_d[qq * 4 + cbd], 5632)
                    Wt = W3(s, 5632, 11)
                    for jl4 in range(4):
                        jc = cbd * 4 + jl4
                        for h in range(2):
                            b = cnt % 8
                            cnt += 1
                            MM([(ps[b][:], Wt[:, kc, jl4 * 128:(jl4 + 1) * 128], RM[:, kc, hs(h)], kc == 0, kc == 10)
                                for kc in range(11)], [f"W{s}"] + [f"RM{k}" for k in range(11)], [f"ps{b}"])
                            rk = RXk([jc], h)
                            STT(RX[:, jc, hs(h)], ps[b][:], modcol[:, 80 + jc:81 + jc], RX[:, jc, hs(h)], ALU.mult,
                                ALU.add, [f"ps{b}", "modcol"] + rk, rk)

        def final(sbi):
            for h in range(2):
                rstd_half(h, scr(h), f"RM{h}", 4 + h, (2, 3))
            for tt in range(8):
                h = tt // 4
                tb = 8 + 4 * (tt % 2)
                tmp = RMf[:, tb * 512:(tb + 4) * 512].rearrange("p (k t) -> p k t", k=16)
                tk = [f"RM{tb + q}" for q in range(4)]
                rs = scr(h)[:, (tt % 4) * 128:(tt % 4 + 1) * 128]
                for kc in range(16):
                    STT(tmp[:, kc, :], RX[:, kc, tt * 128:(tt + 1) * 128], gfin[:, kc:kc + 1], rs, ALU.mult, ALU.mult,
                        [f"RX{kc}t{tt}", "gfin", f"RM{h}"], [tk[kc // 4]])
                st = tt % 4
                ost = RUf[:, st * 2048:(st + 1) * 2048]
                okeys = [f"RU{4 * st + q}" for q in range(4)]
                for g in range(4):
                    b = (tt * 4 + g) % 4
                    TR([(ps[b][:, q * 128:(q + 1) * 128], tmp[:, 4 * g + q, :], ident_f[:]) for q in range(4)],
                       [tk[g], "ident_f"], [f"ps{b}"])
                    COPY(ost[:, g * 512:(g + 1) * 512], ps[b][:], [f"ps{b}"], [okeys[g]], eng="act")
                r0 = sbi * 1024 + tt * 128
                DMA("sp", out_d[r0:r0 + 128, :], ost, okeys, [f"out{sbi}_{tt}"], f"d_o{st}")

        def dump(ap2d, keys, n):
            op("sp", lambda e: e.dma_start(out=dbg_d[:, 0:n], in_=ap2d), reads=keys, writes=["dbg"], dsem="d_dbg")

        def program():
            setup()
            mod(range(0, 11))
            mk_gmod(gmodm, gmix, "gmix", 16, "gmodm")
            loadx(xp, 0)
            norm_to_RH(gmodm, 0, "gmodm")
            mod_pending.extend(range(11, 32))
            vproj(True)
            hgrn_prefix(2)
            loadx(xp, 1, nmod=1)
            norm_to_RH(gmodm, 0, "gmodm")
            ACT(httail[:], RH[:, :, 1022:1024], AF.Copy, RHk(1), ["httail"])
            vproj(True)
            hgrn_prefix(1)
            mod_some(32)
            mk_gmod(gmodf, gffn, "gffn", 64, "gmodf")
            for sbi in range(2):
                loadx(xo, sbi)
                norm_to_RH(gmodm, 0, "gmodm")
                if debug == "h" and sbi == 0:
                    return
                vproj(False)
                hgrn(False, first=(sbi == 0))
                if debug == "u" and sbi == 0:
                    return
                gate(sbi)
                if debug == "m" and sbi == 0:
                    return
                wo()
                if debug == "x1" and sbi == 0:
                    return
                norm_to_RH(gmodf, 48, "gmodf")
                ffn()
                if debug == "x2" and sbi == 0:
                    return
                final(sbi)

        S.plan = True
        try:
            program()
        except StopProgram:
            pass
        S.plan = False
        S.reset()
        try:
            program()
        except StopProgram:
            pass
        if debug:
            if debug == "h":
                op("act", lambda e: e.activation(out=RX[:, 0:8, :], in_=RH[:, 0:8, :], func=AF.Copy),
                   reads=RHk(0) + RHk(1), writes=RXk(range(8), 0) + RXk(range(8), 1))
            if debug in ("u", "c", "v", "s0", "s1", "s2", "s4", "s3a", "s3b", "s3c", "s3x"):
                op("act", lambda e: e.activation(out=RX[:, :, :], in_=RU[:, :, :], func=AF.Copy),
                   reads=[f"RU{k}" for k in range(16)], writes=RXk(range(16), 0) + RXk(range(16), 1))
            if debug == "s3":
                t = [scr(8 + k) for k in range(8)]
                srcs = [(ps[7][:], ["ps7"]), (Qp, ["Qp"]), (Kp, ["Kp"]), (Pm, ["Pm"]), (t[2], ["RM10"]), (t[6], ["RM14"]),
                        (t[1], ["RM9"]), (t[3], ["RM11"]), (t[4], ["RM12"]), (ps[5][:], ["ps5"]), (ps[4][:], ["ps4"])]
                for n_, (a_, k_) in enumerate(srcs):
                    ACT(RX[:, n_, 0:512], a_, AF.Copy, k_, RXk([n_], 0))
                ACT(RX[:, 11, 0:1024], Sm[:].rearrange("p a b -> p (a b)"), AF.Copy, [f"Sm{h}" for h in range(8)], RXk([11], 0) + RXk([11], 1))
            if debug == "m":
                op("act", lambda e: e.activation(out=RX[:, :, :], in_=RM[:, :, :], func=AF.Copy),
                   reads=[f"RM{k}" for k in range(16)], writes=RXk(range(16), 0) + RXk(range(16), 1))
            dump(RX[:].rearrange("p k t -> p (k t)"), RXk(range(16), 0) + RXk(range(16), 1), 16384)
            S.wait_all("sp", ["dbg"])
        else:
            S.wait_all("sp", [f"out{sbi}_{tt}" for sbi in range(2) for tt in range(8)])
        sems = {n: es.enter_context(nc.semaphore(n)) for n in S.sem_names}
        with nc.Block() as block:
            S.replay(block, sems)
    return nc


def _tile_cols(W, col_lists):
    K = W.shape[0]
    kc = K // 128
    Wr = W.reshape(kc, 128, W.shape[1])
    out = []
    for cols in col_lists:
        t = Wr[:, :, cols]
        out.append(np.ascontiguousarray(t.transpose(1, 0, 2)).reshape(128, -1))
    return np.stack(out)


def _col(v):
    return np.ascontiguousarray(v.reshape(-1, 128).T).astype(np.float32)


_CACHE = {}


def _prep_shared(inp):
    D = 2048
    w_in = inp["w_in"][0]
    ar = np.arange
    sh = {}
    wada = inp["w_ada"][0]
    sh["wada"] = _tile_cols(wada, [ar(t * 384, (t + 1) * 384) for t in range(32)])
    sh["gmix"] = _col(inp["norm_mix_g"][0])
    sh["gffn"] = _col(inp["norm_ffn_g"][0])
    sh["gfin"] = _col(inp["norm_final_g"])
    cw = inp["conv_w"][0]
    sh["convw"] = np.ascontiguousarray(cw.reshape(3, 8, 128).transpose(2, 1, 0)).reshape(128, 24)
    lp = inp["lb_param"]
    sh["lbp"] = np.ascontiguousarray(lp.reshape(2, 8, 128).transpose(2, 1, 0)).reshape(128, 16)
    sh["gn"] = np.ascontiguousarray(inp["gnorm_g"][0].reshape(128, 1))
    sh["w_conv"] = _tile_cols(w_in, [np.concatenate([ar(j * 128, (j + 1) * 128), 1024 + ar(j * 128, (j + 1) * 128),
                                                     2048 + ar(j * 128, (j + 1) * 128)]) for j in range(8)])
    sh["w_hq"] = _tile_cols(w_in, [np.concatenate([3072 + ar(j * 128, (j + 1) * 128), 4096 + ar(j * 128, (j + 1) * 128),
                                                   6144 + ar(j * 128, (j + 1) * 128)]) for j in range(8)])
    sh["w_f"] = _tile_cols(w_in, [4096 + ar(j * 128, (j + 1) * 128) for j in range(8)])
    wv = w_in[:, 5120:6144]
    sh["w_v"] = np.stack([_tile_cols(wv[kh * 1024:(kh + 1) * 1024], [ar(cb * 512, (cb + 1) * 512)])[0]
                          for cb in range(2) for kh in range(2)])
    wg = _tile_cols(w_in, [np.concatenate([7168 + ar(j * 128, (j + 1) * 128), 9216 + ar(j * 128, (j + 1) * 128)])
                           for j in range(16)])
    wyo = np.concatenate([inp["w_conv_out"][0], inp["w_hgrn_out"][0]], axis=0)
    wy = _tile_cols(wyo, [ar(j * 128, (j + 1) * 128) for j in range(16)])
    sh["w_gm"] = np.ascontiguousarray(np.concatenate([wg, wy], axis=2))
    sh["w_o"] = _tile_cols(inp["w_o"][0], [ar(j * 256, (j + 1) * 256) for j in range(8)])
    wgate, wup = inp["w_ffn_gate"][0], inp["w_ffn_up"][0]
    wgu = np.concatenate([wgate.reshape(D, 44, 128), wup.reshape(D, 44, 128)], axis=2).reshape(D, 44 * 256)
    sh["w_gu"] = _tile_cols(wgu, [ar(j * 256, (j + 1) * 256) for j in range(44)])
    wd = inp["w_ffn_down"][0]
    sh["w_dn"] = np.stack([_tile_cols(wd[qq * 1408:(qq + 1) * 1408], [ar(cb * 512, (cb + 1) * 512)])[0]
                           for qq in range(4) for cb in range(4)])
    return sh


def _in_maps(inp):
    sh = _prep_shared({k: np.asarray(v, dtype=np.float32) for k, v in inp.items()})
    x = np.asarray(inp["x"], dtype=np.float32)
    c = np.asarray(inp["c"], dtype=np.float32)
    bada = _col(np.asarray(inp["b_ada"], dtype=np.float32)[0])
    zeros = np.zeros((2048, 2048), np.float32)
    maps = []
    for core in range(8):
        b, half = core // 2, core % 2
        m = dict(sh)
        m["xo"] = np.ascontiguousarray(x[b, half * 2048:(half + 1) * 2048])
        m["xp"] = np.ascontiguousarray(x[b, 0:2048]) if half == 1 else zeros
        m["flag"] = np.full((128, 1), float(half), np.float32)
        m["ccol"] = _col(c[b])
        m["badacol"] = bada
        maps.append(m)
    return maps


def kernel(**inputs):
    maps = _in_maps(inputs)
    if "nc" not in _CACHE:
        _CACHE["nc"] = build_nc()
    res = run_bass_kernel_spmd(_CACHE["nc"], maps, core_ids=list(range(8)))
    out = np.empty((4, 4096, 2048), np.float32)
    for core in range(8):
        b, half = core // 2, core % 2
        out[b, half * 2048:(half + 1) * 2048] = res.results[core]["out"]
    return out
```

```python
import os
import numpy as np
from contextlib import ExitStack
import concourse.bass as bass
import concourse.mybir as mybir
from concourse.bass_utils import run_bass_kernel_spmd

F32 = mybir.dt.float32
BF16 = mybir.dt.bfloat16
AF = mybir.ActivationFunctionType
ALU = mybir.AluOpType

ENGS = ["pe", "act", "dve", "pool", "sp"]
SAME_ENGINE_SYNC = True
EPS = 1e-6
BIG = 3.0e38
DEBUG = os.environ.get("MK_DEBUG", "")


class StopProgram(Exception):
    pass


class Sched:
    def __init__(self):
        self.plan = False
        self.reset()

    def reset(self):
        self.streams = {e: [] for e in ENGS}
        self.cnt = {}
        self.seen = {e: {} for e in ENGS}
        self.lastw = {}
        self.readers = {}
        self.sem_names = ["c_" + e for e in ENGS if e != "sp"]

    def op(self, eng, fn, reads=(), writes=(), dsem=None):
        if self.plan:
            return None
        writes = list(writes) + [k for k in reads if k.startswith("ps") and k not in writes]
        need = {}

        def add(tok):
            if tok is None:
                return
            s, v = tok
            if need.get(s, 0) < v:
                need[s] = v

        for k in list(reads) + list(writes):
            add(self.lastw.get(k))
        for k in writes:
            for t in self.readers.get(k, ()):
                add(t)
        own = "c_" + eng
        waits = []
        for s, v in need.items():
            if s == own and (eng == "pe" or not SAME_ENGINE_SYNC):
                continue
            if self.seen[eng].get(s, 0) < v:
                waits.append((s, v))
                self.seen[eng][s] = v
        if dsem is not None:
            if dsem not in self.sem_names:
                self.sem_names.append(dsem)
            self.cnt[dsem] = self.cnt.get(dsem, 0) + 16
            tok = (dsem, self.cnt[dsem])
            inc = 16
        else:
            self.cnt[own] = self.cnt.get(own, 0) + 1
            tok = (own, self.cnt[own])
            inc = 1
        self.streams[eng].append((waits, fn, tok, inc))
        for k in reads:
            self.readers.setdefault(k, []).append(tok)
        for k in writes:
            self.lastw[k] = tok
            self.readers[k] = []
        return tok

    def wait_all(self, eng, keys):
        need = {}
        for k in keys:
            t = self.lastw.get(k)
            if t is not None and need.get(t[0], 0) < t[1]:
                need[t[0]] = t[1]
        self.streams[eng].append((list(need.items()), None, None, 0))

    def replay(self, block, sems):
        engobj = {"pe": "tensor", "act": "scalar", "dve": "vector", "pool": "gpsimd", "sp": "sync"}

        def run(ename):
            def body(eng):
                for waits, fn, tok, inc in self.streams[ename]:
                    for s, v in waits:
                        eng.wait_ge(sems[s], v)
                    if fn is None:
                        continue
                    inst = fn(eng)
                    inst.then_inc(sems[tok[0]], inc)
            return body

        for e in ENGS:
            if self.streams[e]:
                getattr(block, engobj[e])(run(e))


def build_nc(debug=""):
    nc = bass.Bass("TRN2", target_bir_lowering=False, dynamic_dma_scratch_size=4096)

    def din(name, shape):
        return nc.dram_tensor(name, shape, F32, kind="ExternalInput").ap()

    xo = din("xo", [2048, 2048])
    xp = din("xp", [2048, 2048])
    flag_d = din("flag", [128, 1])
    ccol_d = din("ccol", [128, 16])
    wada_d = din("wada", [32, 128, 6144])
    bada_d = din("badacol", [128, 96])
    gmix_d = din("gmix", [128, 16])
    gffn_d = din("gffn", [128, 16])
    gfin_d = din("gfin", [128, 16])
    convw_d = din("convw", [128, 24])
    lbp_d = din("lbp", [128, 16])
    gn_d = din("gn", [128, 1])
    wconv_d = din("w_conv", [8, 128, 6144])
    whq_d = din("w_hq", [8, 128, 6144])
    wf_d = din("w_f", [8, 128, 2048])
    wv_d = din("w_v", [4, 128, 4096])
    wgm_d = din("w_gm", [16, 128, 6144])
    wo_d = din("w_o", [8, 128, 4096])
    wgu_d = din("w_gu", [44, 128, 4096])
    wdn_d = din("w_dn", [16, 128, 5632])
    out_d = nc.dram_tensor("out", [2048, 2048], F32, kind="ExternalOutput").ap()
    dbg_d = None
    if debug:
        dbg_d = nc.dram_tensor("dbg", [128, 16384], F32, kind="ExternalOutput").ap()

    S = Sched()
    with ExitStack() as es:
        def sb(name, shape, dt):
            return es.enter_context(nc.sbuf_tensor(name, shape, dt))

        RX = sb("RX", [128, 16, 1024], F32)
        RH = sb("RH", [128, 16, 1024], BF16)
        RM = sb("RM", [128, 16, 1024], BF16)
        RU = sb("RU", [128, 16, 1024], BF16)
        WR = [sb(f"WR{i}", [128, 6144], BF16) for i in range(3)]
        HG = sb("HG", [128, 2560], BF16)
        ones_f = sb("ones_f", [128, 128], F32)
        ident_f = sb("ident_f", [128, 128], F32)
        ident_b = sb("ident_b", [128, 128], BF16)
        ones_b = sb("ones_b", [128, 128], BF16)
        smask = sb("smask", [128, 512], BF16)
        pmask = sb("pmask", [128, 4, 128], BF16)
        Sm = sb("Sm", [128, 8, 128], F32)
        Sp = sb("Sp", [128, 8, 128], BF16)
        Sh = sb("Sh", [128, 7, 128], F32)
        uhalo = sb("uhalo", [128, 8, 2], F32)
        tl2 = sb("tl2", [128, 2], F32)
        httail = sb("httail", [128, 16, 2], BF16)
        modcol = sb("modcol", [128, 96], F32)
        gmodm = sb("gmodm", [128, 16], F32)
        gmodf = sb("gmodf", [128, 16], F32)
        badac = sb("badac", [128, 96], F32)
        ccol = sb("ccol_s", [128, 16], F32)
        cact = sb("cact", [128, 16], BF16)
        gmix = sb("gmix_s", [128, 16], F32)
        gffn = sb("gffn_s", [128, 16], F32)
        gfin = sb("gfin_s", [128, 16], F32)
        convw = sb("convw_s", [128, 8, 3], F32)
        lbp = sb("lbp_s", [128, 8, 2], F32)
        gn = sb("gn_s", [128, 1], F32)
        flag = sb("flag_s", [128, 1], F32)
        lb = sb("lb", [128, 8], F32)
        oml = sb("oml", [128, 8], F32)
        noml = sb("noml", [128, 8], F32)
        em = sb("em", [128, 8], F32)
        dd = sb("dd", [128, 8], F32)
        epsc = sb("epsc", [128, 1], F32)
        ps = [es.enter_context(nc.psum_tensor(f"ps{i}", [128, 512], F32)) for i in range(8)]

        RMf = RM[:].rearrange("p k t -> p (k t)").bitcast(F32)
        RUf = RU[:].rearrange("p k t -> p (k t)").bitcast(F32)
        HGf = HG[:, 0:2048].bitcast(F32)

        def scr(i, n=512):
            return RMf[:, i * 512:i * 512 + n]

        def scrb(i):
            return RM[:, i, 0:512]

        Qp = HG[:, 0:512]
        Kp = HG[:, 512:1024]
        Ktok = HG[:, 1024:1536].rearrange("p (q k) -> p q k", q=4)
        Pm = HG[:, 1536:2048]
        sqo = HG[:, 2048:2560]

        def hs(h):
            return slice(h * 512, (h + 1) * 512)

        def RXk(kcs, h):
            return [f"RX{k}t{t}" for k in kcs for t in range(4 * h, 4 * h + 4)]

        def RHk(h):
            return [f"RH{k}h{h}" for k in range(16)]

        op = S.op

        def dstop(name):
            if debug == name:
                raise StopProgram()

        def ACT(out, in_, func, reads, writes, **kw):
            op("act", lambda e: e.activation(out=out, in_=in_, func=func, **kw), reads=reads, writes=writes)

        def TT(out, in0, in1, alu, reads, writes):
            op("dve", lambda e: e.tensor_tensor(out=out, in0=in0, in1=in1, op=alu), reads=reads, writes=writes)

        def PTT(out, in0, in1, alu, reads, writes):
            op("pool", lambda e: e.tensor_tensor(out=out, in0=in0, in1=in1, op=alu), reads=reads, writes=writes)

        def TS(out, in0, s1, s2, op0, op1, reads, writes):
            if s2 is None:
                op("dve", lambda e: e.tensor_scalar(out=out, in0=in0, scalar1=s1, scalar2=None, op0=op0),
                   reads=reads, writes=writes)
            else:
                op("dve", lambda e: e.tensor_scalar(out=out, in0=in0, scalar1=s1, scalar2=s2, op0=op0, op1=op1),
                   reads=reads, writes=writes)

        def STT(out, in0, scalar, in1, op0, op1, reads, writes):
            op("dve", lambda e: e.scalar_tensor_tensor(out=out, in0=in0, scalar=scalar, in1=in1, op0=op0, op1=op1),
               reads=reads, writes=writes)

        def MM(mms, reads, writes):
            mms = list(mms)

            def f(e):
                last = None
                for o, l, r, st, sp_ in mms:
                    last = e.matmul(o, lhsT=l, rhs=r, start=st, stop=sp_)
                return last
            op("pe", f, reads=reads, writes=writes)

        def TR(trs, reads, writes):
            trs = list(trs)

            def f(e):
                last = None
                for o, i, idn in trs:
                    last = e.transpose(out=o, in_=i, identity=idn)
                return last
            op("pe", f, reads=reads, writes=writes)

        wlist = []
        wst = {"i": 0, "iss": 0}

        def wload(s, src_k, n_k):
            op("pool", lambda e: e.dma_start(out=WR[s][:, 0:n_k], in_=src_k), writes=[f"W{s}"], dsem=f"d_w{s}")

        released = set()
        manual = set()

        def pump():
            lim = min(len(wlist), wst["i"] + 2)
            while wst["iss"] < lim:
                k = wst["iss"]
                if k >= 3 and (k - 3) not in released:
                    break
                if wlist[k][0] is not None:
                    wload(k % 3, wlist[k][0], wlist[k][1])
                wst["iss"] += 1

        def wtake(src, n, hold=0, man=False):
            if S.plan:
                wlist.append((src, n))
                wst["last"] = len(wlist) - 1
                return 0
            i = wst["i"]
            wst["i"] += 1
            wst["last"] = i
            if man:
                manual.add(i)
            for k in range(max(0, i - 4), i - hold):
                if k not in manual:
                    released.add(k)
            pump()
            assert wst["iss"] > i, ("weight tile not loadable (ring slot still held)", i)
            return i % 3

        def wdone(idx):
            if S.plan:
                return
            released.add(idx)
            pump()

        alt = {"i": 0}

        def COPY(out, in_, reads, writes, eng=None):
            if eng is None:
                alt["i"] += 1
                eng = "act" if alt["i"] % 2 else "dve"
            if eng == "act":
                ACT(out, in_, AF.Copy, reads, writes)
            else:
                op("dve", lambda e: e.tensor_copy(out=out, in_=in_), reads=reads, writes=writes)

        def DMA(eng, out, in_, reads, writes, dsem):
            op(eng, lambda e: e.dma_start(out=out, in_=in_), reads=reads, writes=writes, dsem=dsem)

        def W3(s, n, k):
            return WR[s][:, 0:n].rearrange("p (k c) -> p k c", k=k)

        def setup():
            smalls = [(flag[:], flag_d, "flag"), (ccol[:], ccol_d, "ccol"), (badac[:], bada_d, "badac"),
                      (gmix[:], gmix_d, "gmix"), (gffn[:], gffn_d, "gffn"), (gfin[:], gfin_d, "gfin"),
                      (convw[:].rearrange("p a b -> p (a b)"), convw_d, "convw"),
                      (lbp[:].rearrange("p a b -> p (a b)"), lbp_d, "lbp"), (gn[:], gn_d, "gn")]
            for t, d, k in smalls:
                DMA("sp", t, d, [], [k], "d_c")
            if not S.plan:
                last = S.lastw[smalls[-1][2]]
                for t, d, k in smalls:
                    S.lastw[k] = last

            def P(fn, reads, writes):
                op("pool", fn, reads=reads, writes=writes)
            P(lambda e: e.memset(ones_f[:], 1.0), [], ["ones_f"])
            P(lambda e: e.memset(ones_b[:], 1.0), [], ["ones_b"])
            P(lambda e: e.memset(epsc[:], EPS), [], ["epsc"])
            P(lambda e: e.memset(Sm[:], 0.0), [], [f"Sm{h}" for h in range(8)])
            P(lambda e: e.memset(uhalo[:], 0.0), [], ["uhalo"])
            P(lambda e: e.affine_select(out=ident_f[:], in_=ones_f[:], pattern=[[-1, 128]], compare_op=ALU.is_equal,
                                        fill=0.0, base=0, channel_multiplier=1), ["ones_f"], ["ident_f"])
            P(lambda e: e.tensor_copy(out=ident_b[:], in_=ident_f[:]), ["ident_f"], ["ident_b"])
            ones3 = ones_f[:, 0:64].unsqueeze(1).to_broadcast([128, 8, 64])
            P(lambda e: e.affine_select(out=smask[:].rearrange("p (c t) -> p c t", c=8), in_=ones3,
                                        pattern=[[0, 8], [1, 64]], compare_op=ALU.not_equal, fill=0.0, base=0,
                                        channel_multiplier=0), ["ones_f"], ["smask"])
            onesb = ones_f[:].unsqueeze(1).to_broadcast([128, 4, 128])
            P(lambda e: e.affine_select(out=pmask[:], in_=onesb, pattern=[[0, 4], [1, 128]], compare_op=ALU.is_ge,
                                        fill=0.0, base=0, channel_multiplier=-1), ["ones_f"], ["pmask"])
            P(lambda e: e.affine_select(out=pmask[:, :, 64:128], in_=pmask[:, :, 64:128], pattern=[[0, 4], [0, 64]],
                                        compare_op=ALU.is_ge, fill=0.0, base=-64, channel_multiplier=1),
              ["pmask"], ["pmask"])
            TT(lb[:], lbp[:, :, 0], lbp[:, :, 1], ALU.subtract, ["lbp"], ["lb"])
            ACT(lb[:], lb[:], AF.Sigmoid, ["lb"], ["lb"])
            TS(oml[:], lb[:], -1.0, 1.0, ALU.mult, ALU.add, ["lb"], ["oml"])
            TS(noml[:], lb[:], -1.0, None, ALU.add, None, ["lb"], ["noml"])
            ACT(cact[:], ccol[:], AF.Silu, ["ccol"], ["cact"])

        mod_pending = []

        def mod_some(n, banks=(0, 2)):
            k = min(n, len(mod_pending))
            if k:
                tiles = mod_pending[:k]
                del mod_pending[:k]
                mod(tiles, banks=banks)

        def mod(tiles, banks=(0, 1)):
            for t in tiles:
                s = wtake(wada_d[t], 6144)
                Wt = W3(s, 6144, 16)
                b = banks[t % 2]
                MM([(ps[b][:, g:g + 1], Wt[:, kc, g * 128:(g + 1) * 128], cact[:, kc:kc + 1], kc == 0, kc == 15)
                    for g in range(3) for kc in range(16)], [f"W{s}", "cact"], [f"ps{b}"])
                TT(modcol[:, 3 * t:3 * t + 3], ps[b][:, 0:3], badac[:, 3 * t:3 * t + 3], ALU.add,
                   [f"ps{b}", "badac"], ["modcol"])

        def mk_gmod(dst, g, key_g, c0, key):
            STT(dst[:], modcol[:, c0:c0 + 16], 1.0, g[:], ALU.add, ALU.mult, ["modcol", key_g], [key])

        def loadx(src, sbi, nmod=0):
            for tt in range(8):
                if nmod:
                    mod_some(nmod, banks=(6, 7))
                st = tt % 2
                xst = RUf[:, st * 2048:(st + 1) * 2048]
                skeys = [f"RU{4 * st + q}" for q in range(4)]
                r0 = sbi * 1024 + tt * 128
                DMA("sp", xst, src[r0:r0 + 128, :], [], skeys, f"d_x{st}")
                for g in range(4):
                    b = (tt * 4 + g) % 4
                    TR([(ps[b][:, q * 128:(q + 1) * 128], xst[:, (4 * g + q) * 128:(4 * g + q + 1) * 128], ident_f[:])
                        for q in range(4)], skeys + ["ident_f"], [f"ps{b}"])
                    COPY(RX[:, 4 * g:4 * g + 4, tt * 128:(tt + 1) * 128], ps[b][:].rearrange("p (q t) -> p q t", q=4),
                         [f"ps{b}"], [f"RX{4 * g + q}t{tt}" for q in range(4)])

        def rstd_half(h, dst, dkey, pb, sq_slots):
            for kc in range(16):
                sl = sq_slots[kc % 2]
                xin = RX[:, kc, hs(h)]
                if kc % 4 == 3:
                    ACT(scrb(sl), xin, AF.Square, RXk([kc], h), [f"RM{sl}"])
                else:
                    PTT(scrb(sl), xin, xin, ALU.mult, RXk([kc], h), [f"RM{sl}"])
                MM([(ps[pb][:], ones_b[:], scrb(sl), kc == 0, kc == 15)], [f"RM{sl}", "ones_b"], [f"ps{pb}"])
            ACT(dst, ps[pb][:], AF.Ln, [f"ps{pb}", "epsc"], [dkey], scale=1.0 / 2048.0, bias=epsc[:])
            ACT(dst, dst, AF.Exp, [dkey], [dkey], scale=-0.5)

        def norm_to_RH(gmod, sh_c0, gkey):
            for h in range(2):
                rs = scr(10 + h)
                rstd_half(h, rs, f"RM{10 + h}", 4 + h, (12, 13))
                for kc in range(16):
                    ts = 14 + kc % 2
                    TT(scr(ts), RX[:, kc, hs(h)], rs, ALU.mult, RXk([kc], h) + [f"RM{10 + h}"], [f"RM{ts}"])
                    ACT(RH[:, kc, hs(h)], scr(ts), AF.Identity, [f"RM{ts}", gkey, "modcol"], [f"RH{kc}h{h}"],
                        scale=gmod[:, kc:kc + 1], bias=modcol[:, sh_c0 + kc:sh_c0 + kc + 1])

        RXf = RX[:].rearrange("p k t -> p (k t)")
        conv_state = {}

        def conv_item(k, first=False):
            j, h = k // 2, k % 2
            so = 4 * (k % 2)
            def sl(i, n=512):
                return RXf[:, (so + i) * 1024:(so + i) * 1024 + n]
            def sk(i):
                return RXk([so + i], 0) + RXk([so + i], 1)
            t1, ub, y1, y2 = sl(0), sl(1, 514), sl(2), sl(3)
            T1, UB, Y1, Y2 = sk(0), sk(1), sk(2), sk(3)
            if h == 0:
                conv_state["s"] = wtake(wconv_d[j], 6144, man=True)
                conv_state["idx"] = wst["last"]
            s = conv_state["s"]
            Wt = W3(s, 6144, 16)
            if first and h == 0:
                MM([(ps[3][:, 2 * g:2 * g + 2], Wt[:, kc, (g + 1) * 128:(g + 2) * 128], httail[:, kc, :], kc == 0,
                     kc == 15) for g in range(2) for kc in range(16)], [f"W{s}", "httail"], ["ps3"])
                ACT(tl2[:], ps[3][:, 0:2], AF.Copy, ["ps3"], ["tl2"])
                STT(uhalo[:, j, :], ps[3][:, 2:4], flag[:, 0:1], tl2[:], ALU.mult, ALU.mult, ["ps3", "flag", "tl2"],
                    ["uhalo"])
            pb = 3
            MM([(ps[pb + g][:], Wt[:, kc, g * 128:(g + 1) * 128], RH[:, kc, hs(h)], kc == 0, kc == 15)
                for g in (1, 2) for kc in range(16)], [f"W{s}"] + RHk(h), [f"ps{pb + 1}", f"ps{pb + 2}"])
            ACT(t1, ps[pb + 1][:], AF.Copy, [f"ps{pb + 1}"], T1)
            ACT(ub[:, 0:2], uhalo[:, j, :], AF.Copy, ["uhalo"], UB)
            TT(ub[:, 2:514], ps[pb + 2][:], t1, ALU.mult, [f"ps{pb + 2}"] + T1, UB)
            ACT(uhalo[:, j, :], ub[:, 512:514], AF.Copy, UB, ["uhalo"])
            TS(y1, ub[:, 0:512], convw[:, j, 0:1], None, ALU.mult, None, UB + ["convw"], Y1)
            STT(y2, ub[:, 1:513], convw[:, j, 1:2], y1, ALU.mult, ALU.add, UB + ["convw"] + Y1, Y2)
            STT(y1, ub[:, 2:514], convw[:, j, 2:3], y2, ALU.mult, ALU.add, UB + ["convw"] + Y2, Y1)
            conv_state["fin"] = (Wt, s, j, h, y1, Y1)

        def conv_item_b_mm():
            Wt, s, j, h, y1, Y1 = conv_state["fin"]
            MM([(ps[3][:], Wt[:, kc, 0:128], RH[:, kc, hs(h)], kc == 0, kc == 15) for kc in range(16)],
               [f"W{s}"] + RHk(h), ["ps3"])
            if h == 1:
                wdone(conv_state["idx"])

        def conv_item_b_tt():
            Wt, s, j, h, y1, Y1 = conv_state["fin"]
            TT(RU[:, j, hs(h)], ps[3][:], y1, ALU.mult, ["ps3"] + Y1, [f"RU{j}"])

        def reloadx_start(src, sbi, tt):
            s = wtake(None, 4096, man=True)
            idx = wst["last"]
            stage = WR[s][:, 0:4096].bitcast(F32)
            r0 = sbi * 1024 + tt * 128
            DMA("sp", stage, src[r0:r0 + 128, :], [], [f"W{s}"], f"d_r{s}")
            return (s, idx, stage, tt)

        def reloadx_finish(st, bank0=4):
            s, idx, stage, tt = st
            for g in range(4):
                b = bank0 + g
                TR([(ps[b][:, q * 128:(q + 1) * 128], stage[:, (4 * g + q) * 128:(4 * g + q + 1) * 128], ident_f[:])
                    for q in range(4)], [f"W{s}", "ident_f"], [f"ps{b}"])
                COPY(RX[:, 4 * g:4 * g + 4, tt * 128:(tt + 1) * 128], ps[b][:].rearrange("p (q t) -> p q t", q=4),
                     [f"ps{b}"], [f"RX{4 * g + q}t{tt}" for q in range(4)])
            wdone(idx)

        def conv_tail():
            for j in range(8):
                s = wtake(wconv_d[j], 6144)
                Wt = W3(s, 6144, 16)
                b = 6 + j % 2
                MM([(ps[b][:, 2 * g:2 * g + 2], Wt[:, kc, (g + 1) * 128:(g + 2) * 128], RH[:, kc, 1022:1024], kc == 0,
                     kc == 15) for g in range(2) for kc in range(16)], [f"W{s}"] + RHk(1), [f"ps{b}"])
                ACT(tl2[:], ps[b][:, 0:2], AF.Copy, [f"ps{b}"], ["tl2"])
                STT(uhalo[:, j, :], ps[b][:, 2:4], flag[:, 0:1], tl2[:], ALU.mult, ALU.mult, [f"ps{b}", "flag", "tl2"],
                    ["uhalo"])

        def vproj(prefix):
            for cb in range(2):
                s0 = wtake(wv_d[cb * 2], 4096)
                s1 = wtake(wv_d[cb * 2 + 1], 4096, hold=1)
                Ws = [W3(s0, 4096, 8), W3(s1, 4096, 8)]
                for tt in range(8):
                    b = 3 + (cb * 8 + tt) % 4
                    MM([(ps[b][:], RH[:, kc, tt * 128:(tt + 1) * 128], Ws[kc // 8][:, kc % 8, :], kc == 0, kc == 15)
                        for kc in range(16)], [f"W{s0}", f"W{s1}"] + RHk(tt // 4), [f"ps{b}"])
                    dst = RM[:, tt, cb * 512:(cb + 1) * 512]
                    if prefix:
                        ACT(dst, ps[b][:], AF.Copy, [f"ps{b}", "flag"], [f"RM{tt}"], scale=flag[:, 0:1])
                    else:
                        COPY(dst, ps[b][:], [f"ps{b}"], [f"RM{tt}"])

        def hgrn(prefix, first=False):
            items = [(hd, h) for hd in range(8) for h in range(2)]
            n_it = len(items)
            wslot = {}
            widx = {}
            L = 2 if prefix else 1
            t = [scr(8 + k) for k in range(8)]
            K = [f"RM{8 + k}" for k in range(8)]
            X, Y = 6, 7
            psXb = ps[X][:].bitcast(BF16)
            bc3 = t[2].rearrange("p (c t) -> p c t", c=8)
            bm3 = t[6].rearrange("p (c t) -> p c t", c=8)

            def bigpart(i, part):
                if i >= n_it:
                    return
                hd, h = items[i]
                s3 = 3 * (i % 2) if prefix else 0
                if part == 0 and h == 0:
                    if prefix:
                        wslot[hd] = wtake(wf_d[hd], 2048)
                    else:
                        wslot[hd] = wtake(whq_d[hd], 6144, man=True)
                        widx[hd] = wst["last"]
                s = wslot[hd]
                if prefix:
                    if part != 0:
                        return
                    Wt = W3(s, 2048, 16)
                    MM([(ps[s3 + 1][:], Wt[:, kc, :], RH[:, kc, hs(h)], kc == 0, kc == 15) for kc in range(16)],
                       [f"W{s}"] + RHk(h), [f"ps{s3 + 1}"])
                else:
                    Wt = W3(s, 6144, 16)
                    g = part
                    MM([(ps[s3 + g][:], Wt[:, kc, g * 128:(g + 1) * 128], RH[:, kc, hs(h)], kc == 0, kc == 15)
                        for kc in range(16)], [f"W{s}"] + RHk(h), [f"ps{s3 + g}"])
                    if part == 2 and h == 1:
                        wdone(widx[hd])

            def elem(i):
                hd, h = items[i]
                s3 = 3 * (i % 2) if prefix else 0
                pq, pf, pg = ps[s3], ps[s3 + 1], ps[s3 + 2]
                ACT(t[0], pf[:], AF.Sigmoid, [f"ps{s3 + 1}"], [K[0]])
                if not prefix:
                    ACT(t[4], pq[:], AF.Silu, [f"ps{s3}"], [K[4]])
                    ACT(t[5], pg[:], AF.Silu, [f"ps{s3 + 2}"], [K[5]])
                ACT(t[1], t[0], AF.Ln, [K[0], "oml", "lb"], [K[1]], scale=oml[:, hd:hd + 1], bias=lb[:, hd:hd + 1])
                op("dve", lambda e: e.tensor_tensor_scan(out=t[2], data0=smask[:], data1=t[1], initial=0.0,
                                                         op0=ALU.mult, op1=ALU.add),
                   reads=[K[1], "smask"], writes=[K[2]])
                TS(t[3], t[0], noml[:, hd:hd + 1], oml[:, hd:hd + 1], ALU.mult, ALU.add, [K[0], "noml", "oml"], [K[3]])
                ACT(dd[:], bc3[:, :, 63], AF.Exp, [K[2]], ["dd"])
                if not prefix:
                    ACT(em[:], bc3[:, :, 32], AF.Exp, [K[2]], ["em"])
                TT(bm3, bc3, bc3[:, :, 32:33].to_broadcast([128, 8, 64]), ALU.subtract, [K[2]], [K[6]])
                ACT(t[1], t[6], AF.Exp, [K[6]], [K[1]], scale=-1.0)
                if prefix:
                    ACT(bm3[:, :, 63], bm3[:, :, 63], AF.Exp, [K[6]], [K[6]])
                else:
                    ACT(t[6], t[6], AF.Exp, [K[6]], [K[6]])
                if not prefix:
                    TT(Qp, t[4], t[6], ALU.mult, [K[4], K[6]], ["Qp"])
                TT(Kp, t[3], t[1], ALU.mult, [K[3], K[1]], ["Kp"])

            def small(i):
                hd, h = items[i]
                TR([(psXb[:, q * 128:(q + 1) * 128], Kp[:, q * 128:(q + 1) * 128], ident_b[:]) for q in range(4)],
                   ["Kp", "ident_b"], [f"ps{X}"])
                ACT(Ktok, psXb[:, 0:512].rearrange("p (q k) -> p q k", q=4), AF.Copy, [f"ps{X}"], ["Ktok"])
                if not prefix:
                    MM([(ps[Y][:, q * 128:(q + 1) * 128], Kp[:, q * 128:(q + 1) * 128], Qp[:, q * 128:(q + 1) * 128],
                         True, True) for q in range(4)], ["Kp", "Qp"], [f"ps{Y}"])
                    STT(Pm, ps[Y][:], BIG, pmask[:].rearrange("p a b -> p (a b)"), ALU.min, ALU.mult,
                        [f"ps{Y}", "pmask"], ["Pm"])
                    dstop("s1")
                bigpart(i + L, 0)
                vt = [f"RM{4 * h + q}" for q in range(4)]
                AB = [X, Y]
                MM([(ps[AB[c % 2]][:, (c // 2) * 128:(c // 2 + 1) * 128],
                     Ktok[(c % 2) * 64:(c % 2) * 64 + 64, c // 2, :],
                     RM[(c % 2) * 64:(c % 2) * 64 + 64, 4 * h + c // 2, hd * 128:(hd + 1) * 128], True, True)
                    for c in range(8)], ["Ktok"] + vt, [f"ps{X}", f"ps{Y}"])
                bigpart(i + L, 1)
                if not prefix:
                    conv_item_b_mm()
                t6q = t[6].rearrange("p (q two t) -> p q two t", two=2, t=64)
                for par in range(2):
                    pv = ps[AB[par]][:].rearrange("p (q v) -> p q v", q=4)
                    TT(pv, pv, t6q[:, :, par, 63:64].to_broadcast([128, 4, 128]), ALU.mult,
                       [f"ps{AB[par]}", K[6]], [f"ps{AB[par]}"])
                for c in range(8):
                    srcS = Sm[:, hd, :] if c == 0 else Sh[:, c - 1, :]
                    skey = f"Sm{hd}" if c == 0 else f"Sh{c - 1}"
                    dstS = Sm[:, hd, :] if c == 7 else Sh[:, c, :]
                    dkey = f"Sm{hd}" if c == 7 else f"Sh{c}"
                    if not prefix:
                        ACT(Sp[:, c, :], srcS, AF.Copy, [skey, "em"], [f"Sp{c}"], scale=em[:, c:c + 1])
                    STT(dstS, srcS, dd[:, c:c + 1], ps[AB[c % 2]][:, (c // 2) * 128:(c // 2 + 1) * 128],
                        ALU.mult, ALU.add, [skey, "dd", f"ps{AB[c % 2]}"], [dkey])
                if prefix:
                    bigpart(i + L, 2)
                    return
                dstop("s2")
                mms = []
                for q in range(4):
                    mms.append((ps[X][:, q * 128:(q + 1) * 128], RM[:, 4 * h + q, hd * 128:(hd + 1) * 128],
                                Pm[:, q * 128:(q + 1) * 128], True, False))
                    for cc in range(2):
                        c = 2 * q + cc
                        mms.append((ps[X][:, c * 64:(c + 1) * 64], Sp[:, c, :], Qp[:, c * 64:(c + 1) * 64], False,
                                    cc == 1))
                MM(mms, vt + ["Pm", "Qp"] + [f"Sp{c}" for c in range(8)], [f"ps{X}"])
                bigpart(i + L, 2)
                conv_item_b_tt()
                dstop("s3")
                ACT(sqo, ps[X][:], AF.Square, [f"ps{X}"], ["sqo"])
                TT(t[7], ps[X][:], t[5], ALU.mult, [f"ps{X}", K[5]], [K[7]])
                MM([(ps[Y][:], ones_b[:], sqo, True, True)], ["sqo", "ones_b"], [f"ps{Y}"])
                ACT(t[0], ps[Y][:], AF.Ln, [f"ps{Y}", "epsc"], [K[0]], scale=1.0 / 128.0, bias=epsc[:])
                ACT(t[0], t[0], AF.Exp, [K[0]], [K[0]], scale=-0.5)
                STT(RU[:, 8 + hd, hs(h)], t[7], gn[:, 0:1], t[0], ALU.mult, ALU.mult, [K[7], "gn", K[0]],
                    [f"RU{8 + hd}"])

            for i0 in range(L):
                for part in range(3):
                    bigpart(i0, part)
            for i in range(n_it):
                elem(i)
                if prefix and items[i][1] == 0:
                    mod_some(prefix)
                if not prefix:
                    conv_item(i, first)
                small(i)
                if not prefix:
                    dstop("s4")

        def hgrn_prefix(nmod):
            def rxs(n, w=512):
                return RXf[:, n * 1024:n * 1024 + w], RXk([n], 0) + RXk([n], 1)
            pipes = []
            for p in range(2):
                if p == 0:
                    tl = [(scr(8 + k), [f"RM{8 + k}"]) for k in range(5)]
                    bs = dict(t=tl, Kp=(Kp, ["Kp"]), Ktok=(Ktok, ["Ktok"]), dd=(dd, ["dd"]),
                              Sh=(Sh, [f"Sh{c}" for c in range(7)]), F=1, X=6, Y=7)
                else:
                    tl = [rxs(k) for k in range(5)]
                    shv, shk = rxs(5, 896)
                    bs = dict(t=tl, Kp=(Qp, ["Qp"]), Ktok=(HG[:, 1536:2048].rearrange("p (q k) -> p q k", q=4), ["Pm"]),
                              dd=(em, ["em"]), Sh=(shv.rearrange("p (c v) -> p c v", c=7), shk), F=4, X=3, Y=5)
                bs["heads"] = [0, 2, 4, 6] if p == 0 else [1, 3, 5, 7]
                pipes.append(bs)

            def proj(bs, hd, h):
                if h == 0:
                    bs["s"] = wtake(wf_d[hd], 2048, man=True)
                    bs["idx"] = wst["last"]
                s = bs["s"]
                Wt = W3(s, 2048, 16)
                F = bs["F"]
                MM([(ps[F][:], Wt[:, kc, :], RH[:, kc, hs(h)], kc == 0, kc == 15) for kc in range(16)],
                   [f"W{s}"] + RHk(h), [f"ps{F}"])
                if h == 1:
                    wdone(bs["idx"])

            def st1(bs, hd, h):
                (sg, ksg), (lf, klf), (bc, kbc), (kk, kkk), (bm, kbm) = bs["t"]
                F = bs["F"]
                ACT(sg, ps[F][:], AF.Sigmoid, [f"ps{F}"], ksg)
                ACT(lf, sg, AF.Ln, ksg + ["oml", "lb"], klf, scale=oml[:, hd:hd + 1], bias=lb[:, hd:hd + 1])
                op("dve", lambda e: e.tensor_tensor_scan(out=bc, data0=smask[:], data1=lf, initial=0.0,
                                                         op0=ALU.mult, op1=ALU.add),
                   reads=klf + ["smask"], writes=kbc)
                TS(kk, sg, noml[:, hd:hd + 1], oml[:, hd:hd + 1], ALU.mult, ALU.add, ksg + ["noml", "oml"], kkk)
                bc3 = bc.rearrange("p (c t) -> p c t", c=8)
                ddv, kdd = bs["dd"]
                ACT(ddv[:], bc3[:, :, 63], AF.Exp, kbc, kdd)

            def st2(bs, hd, h):
                (sg, ksg), (lf, klf), (bc, kbc), (kk, kkk), (bm, kbm) = bs["t"]
                bc3 = bc.rearrange("p (c t) -> p c t", c=8)
                bm3 = bm.rearrange("p (c t) -> p c t", c=8)
                TT(bm3, bc3, bc3[:, :, 32:33].to_broadcast([128, 8, 64]), ALU.subtract, kbc, kbm)
                ACT(lf, bm, AF.Exp, kbm, klf, scale=-1.0)
                ACT(bm3[:, :, 63], bm3[:, :, 63], AF.Exp, kbm, kbm)
                kpv, kkp = bs["Kp"]
                TT(kpv, kk, lf, ALU.mult, kkk + klf, kkp)

            def st3(bs, hd, h):
                kpv, kkp = bs["Kp"]
                ktv, kkt = bs["Ktok"]
                X = bs["X"]
                psXb_ = ps[X][:].bitcast(BF16)
                TR([(psXb_[:, q * 128:(q + 1) * 128], kpv[:, q * 128:(q + 1) * 128], ident_b[:]) for q in range(4)],
                   kkp + ["ident_b"], [f"ps{X}"])
                ACT(ktv, psXb_[:, 0:512].rearrange("p (q k) -> p q k", q=4), AF.Copy, [f"ps{X}"], kkt)

            def st4(bs, hd, h):
                (sg, ksg), (lf, klf), (bc, kbc), (kk, kkk), (bm, kbm) = bs["t"]
                ktv, kkt = bs["Ktok"]
                AB = [bs["X"], bs["Y"]]
                vt = [f"RM{4 * h + q}" for q in range(4)]
                MM([(ps[AB[c % 2]][:, (c // 2) * 128:(c // 2 + 1) * 128],
                     ktv[(c % 2) * 64:(c % 2) * 64 + 64, c // 2, :],
                     RM[(c % 2) * 64:(c % 2) * 64 + 64, 4 * h + c // 2, hd * 128:(hd + 1) * 128], True, True)
                    for c in range(8)], kkt + vt, [f"ps{AB[0]}", f"ps{AB[1]}"])
                bmq = bm.rearrange("p (q two t) -> p q two t", two=2, t=64)
                for par in range(2):
                    pv = ps[AB[par]][:].rearrange("p (q v) -> p q v", q=4)
                    TT(pv, pv, bmq[:, :, par, 63:64].to_broadcast([128, 4, 128]), ALU.mult,
                       [f"ps{AB[par]}"] + kbm, [f"ps{AB[par]}"])

            def st5(bs, hd, h):
                AB = [bs["X"], bs["Y"]]
                shv, shk = bs["Sh"]
                ddv, kdd = bs["dd"]
                for c in range(8):
                    srcS = Sm[:, hd, :] if c == 0 else shv[:, c - 1, :]
                    skey = [f"Sm{hd}"] if c == 0 else shk
                    dstS = Sm[:, hd, :] if c == 7 else shv[:, c, :]
                    dkey = [f"Sm{hd}"] if c == 7 else shk
                    STT(dstS, srcS, ddv[:, c:c + 1], ps[AB[c % 2]][:, (c // 2) * 128:(c // 2 + 1) * 128],
                        ALU.mult, ALU.add, skey + kdd + [f"ps{AB[c % 2]}"], dkey)

            seq = [[(hd, h) for hd in bs["heads"] for h in range(2)] for bs in pipes]
            for p in range(2):
                proj(pipes[p], *seq[p][0])
            for n in range(8):
                for stage in (st1, st2, st3, st4, st5):
                    for p in range(2):
                        stage(pipes[p], *seq[p][n])
                    if stage is st1 and n + 1 < 8:
                        for p in range(2):
                            proj(pipes[p], *seq[p][n + 1])
                        if n % 2 == 0:
                            mod_some(nmod)

        def gate(sbi):
            pend = []
            sa = HG[:, 1024:1536]
            sbb = HG[:, 1536:2048]
            m1 = HGf[:, 0:512]
            for j in range(16):
                if j % 2 == 0:
                    pend.append(reloadx_start(xo, sbi, j // 2))
                s = wtake(wgm_d[j], 6144, man=True)
                gidx = wst["last"]
                Wg = WR[s][:, 0:4096].rearrange("p (k c) -> p k c", k=16)
                Wy = WR[s][:, 4096:6144].rearrange("p (k c) -> p k c", k=16)
                for h in range(2):
                    pb = 4 * ((j * 2 + h) % 2)
                    mms = [(ps[pb + g][:], Wg[:, kc, g * 128:(g + 1) * 128], RH[:, kc, hs(h)], kc == 0, kc == 15)
                           for g in range(2) for kc in range(16)]
                    mms += [(ps[pb + 2 + g][:], Wy[:, 8 * g + kc, :], RU[:, 8 * g + kc, hs(h)], kc == 0, kc == 7)
                            for g in range(2) for kc in range(8)]
                    MM(mms, [f"W{s}"] + RHk(h) + [f"RU{k}" for k in range(16)], [f"ps{pb + g}" for g in range(4)])
                    if h == 1:
                        wdone(gidx)
                    ACT(sa, ps[pb][:], AF.Sigmoid, [f"ps{pb}"], ["Ktok"])
                    ACT(sbb, ps[pb + 1][:], AF.Sigmoid, [f"ps{pb + 1}"], ["Pm"])
                    TT(m1, ps[pb + 2][:], sa, ALU.mult, [f"ps{pb + 2}", "Ktok"], ["Qp", "Kp"])
                    TT(ps[pb][:], ps[pb + 3][:], sbb, ALU.mult, [f"ps{pb + 3}", "Pm"], [f"ps{pb}"])
                    TT(RM[:, j, hs(h)], ps[pb][:], m1, ALU.add, [f"ps{pb}", "Qp", "Kp"], [f"RM{j}"])
                    if h == 0 and pend:
                        reloadx_finish(pend.pop(0))
            while pend:
                reloadx_finish(pend.pop(0))

        cntr = {"b": 0}

        def wo():
            for jp in range(8):
                s = wtake(wo_d[jp], 4096)
                Wt = W3(s, 4096, 16)
                for jl in range(2):
                    jc = 2 * jp + jl
                    for h in range(2):
                        b = cntr["b"] % 8
                        cntr["b"] += 1
                        MM([(ps[b][:], Wt[:, kc, jl * 128:(jl + 1) * 128], RM[:, kc, hs(h)], kc == 0, kc == 15)
                            for kc in range(16)], [f"W{s}"] + [f"RM{k}" for k in range(16)], [f"ps{b}"])
                        rk = RXk([jc], h)
                        STT(RX[:, jc, hs(h)], ps[b][:], modcol[:, 32 + jc:33 + jc], RX[:, jc, hs(h)], ALU.mult, ALU.add,
                            [f"ps{b}", "modcol"] + rk, rk)

        def ffn():
            sgb = [HGf[:, 0:512], HGf[:, 512:1024]]
            sgk = [["Qp", "Kp"], ["Ktok", "Pm"]]
            cnt = 0
            for qq in range(4):
                for jl in range(11):
                    jj = qq * 11 + jl
                    s = wtake(wgu_d[jj], 4096)
                    Wt = W3(s, 4096, 16)
                    for h in range(2):
                        pb = 2 * (cnt % 4)
                        sg = sgb[cnt % 2]
                        sk = sgk[cnt % 2]
                        cnt += 1
                        MM([(ps[pb + g][:], Wt[:, kc, g * 128:(g + 1) * 128], RH[:, kc, hs(h)], kc == 0, kc == 15)
                            for g in range(2) for kc in range(16)], [f"W{s}"] + RHk(h), [f"ps{pb}", f"ps{pb + 1}"])
                        ACT(sg, ps[pb][:], AF.Silu, [f"ps{pb}"], sk)
                        TT(RM[:, jl, hs(h)], ps[pb + 1][:], sg, ALU.mult, [f"ps{pb + 1}"] + sk, [f"RM{jl}"])
                for cbd in range(4):
                    s = wtake(wdn_d[qq * 4 + cbd], 5632)
                    Wt = W3(s, 5632, 11)
                    for jl4 in range(4):
                        jc = cbd * 4 + jl4
                        for h in range(2):
                            b = cnt % 8
                            cnt += 1
                            MM([(ps[b][:], Wt[:, kc, jl4 * 128:(jl4 + 1) * 128], RM[:, kc, hs(h)], kc == 0, kc == 10)
                                for kc in range(11)], [f"W{s}"] + [f"RM{k}" for k in range(11)], [f"ps{b}"])
                            rk = RXk([jc], h)
                            STT(RX[:, jc, hs(h)], ps[b][:], modcol[:, 80 + jc:81 + jc], RX[:, jc, hs(h)], ALU.mult,
                                ALU.add, [f"ps{b}", "modcol"] + rk, rk)

        def final(sbi):
            for h in range(2):
                rstd_half(h, scr(h), f"RM{h}", 4 + h, (2, 3))
            for tt in range(8):
                h = tt // 4
                tb = 8 + 4 * (tt % 2)
                tmp = RMf[:, tb * 512:(tb + 4) * 512].rearrange("p (k t) -> p k t", k=16)
                tk = [f"RM{tb + q}" for q in range(4)]
                rs = scr(h)[:, (tt % 4) * 128:(tt % 4 + 1) * 128]
                for kc in range(16):
                    STT(tmp[:, kc, :], RX[:, kc, tt * 128:(tt + 1) * 128], gfin[:, kc:kc + 1], rs, ALU.mult, ALU.mult,
                        [f"RX{kc}t{tt}", "gfin", f"RM{h}"], [tk[kc // 4]])
                st = tt % 4
                ost = RUf[:, st * 2048:(st + 1) * 2048]
                okeys = [f"RU{4 * st + q}" for q in range(4)]
                for g in range(4):
                    b = (tt * 4 + g) % 4
                    TR([(ps[b][:, q * 128:(q + 1) * 128], tmp[:, 4 * g + q, :], ident_f[:]) for q in range(4)],
                       [tk[g], "ident_f"], [f"ps{b}"])
                    COPY(ost[:, g * 512:(g + 1) * 512], ps[b][:], [f"ps{b}"], [okeys[g]], eng="act")
                r0 = sbi * 1024 + tt * 128
                DMA("sp", out_d[r0:r0 + 128, :], ost, okeys, [f"out{sbi}_{tt}"], f"d_o{st}")

        def dump(ap2d, keys, n):
            op("sp", lambda e: e.dma_start(out=dbg_d[:, 0:n], in_=ap2d), reads=keys, writes=["dbg"], dsem="d_dbg")

        def program():
            setup()
            mod(range(0, 11))
            mk_gmod(gmodm, gmix, "gmix", 16, "gmodm")
            loadx(xp, 0)
            norm_to_RH(gmodm, 0, "gmodm")
            mod_pending.extend(range(11, 32))
            vproj(True)
            hgrn_prefix(2)
            loadx(xp, 1, nmod=1)
            norm_to_RH(gmodm, 0, "gmodm")
            ACT(httail[:], RH[:, :, 1022:1024], AF.Copy, RHk(1), ["httail"])
            vproj(True)
            hgrn_prefix(1)
            mod_some(32)
            mk_gmod(gmodf, gffn, "gffn", 64, "gmodf")
            for sbi in range(2):
                loadx(xo, sbi)
                norm_to_RH(gmodm, 0, "gmodm")
                if debug == "h" and sbi == 0:
                    return
                vproj(False)
                hgrn(False, first=(sbi == 0))
                if debug == "u" and sbi == 0:
                    return
                gate(sbi)
                if debug == "m" and sbi == 0:
                    return
                wo()
                if debug == "x1" and sbi == 0:
                    return
                norm_to_RH(gmodf, 48, "gmodf")
                ffn()
                if debug == "x2" and sbi == 0:
                    return
                final(sbi)

        S.plan = True
        try:
            program()
        except StopProgram:
            pass
        S.plan = False
        S.reset()
        try:
            program()
        except StopProgram:
            pass
        if debug:
            if debug == "h":
                op("act", lambda e: e.activation(out=RX[:, 0:8, :], in_=RH[:, 0:8, :], func=AF.Copy),
                   reads=RHk(0) + RHk(1), writes=RXk(range(8), 0) + RXk(range(8), 1))
            if debug in ("u", "c", "v", "s0", "s1", "s2", "s4", "s3a", "s3b", "s3c", "s3x"):
                op("act", lambda e: e.activation(out=RX[:, :, :], in_=RU[:, :, :], func=AF.Copy),
                   reads=[f"RU{k}" for k in range(16)], writes=RXk(range(16), 0) + RXk(range(16), 1))
            if debug == "s3":
                t = [scr(8 + k) for k in range(8)]
                srcs = [(ps[7][:], ["ps7"]), (Qp, ["Qp"]), (Kp, ["Kp"]), (Pm, ["Pm"]), (t[2], ["RM10"]), (t[6], ["RM14"]),
                        (t[1], ["RM9"]), (t[3], ["RM11"]), (t[4], ["RM12"]), (ps[5][:], ["ps5"]), (ps[4][:], ["ps4"])]
                for n_, (a_, k_) in enumerate(srcs):
                    ACT(RX[:, n_, 0:512], a_, AF.Copy, k_, RXk([n_], 0))
                ACT(RX[:, 11, 0:1024], Sm[:].rearrange("p a b -> p (a b)"), AF.Copy, [f"Sm{h}" for h in range(8)], RXk([11], 0) + RXk([11], 1))
            if debug == "m":
                op("act", lambda e: e.activation(out=RX[:, :, :], in_=RM[:, :, :], func=AF.Copy),
                   reads=[f"RM{k}" for k in range(16)], writes=RXk(range(16), 0) + RXk(range(16), 1))
            dump(RX[:].rearrange("p k t -> p (k t)"), RXk(range(16), 0) + RXk(range(16), 1), 16384)
            S.wait_all("sp", ["dbg"])
        else:
            S.wait_all("sp", [f"out{sbi}_{tt}" for sbi in range(2) for tt in range(8)])
        sems = {n: es.enter_context(nc.semaphore(n)) for n in S.sem_names}
        with nc.Block() as block:
            S.replay(block, sems)
    return nc


def _tile_cols(W, col_lists):
    K = W.shape[0]
    kc = K // 128
    Wr = W.reshape(kc, 128, W.shape[1])
    out = []
    for cols in col_lists:
        t = Wr[:, :, cols]
        out.append(np.ascontiguousarray(t.transpose(1, 0, 2)).reshape(128, -1))
    return np.stack(out)


def _col(v):
    return np.ascontiguousarray(v.reshape(-1, 128).T).astype(np.float32)


_CACHE = {}


def _prep_shared(inp):
    D = 2048
    w_in = inp["w_in"][0]
    ar = np.arange
    sh = {}
    wada = inp["w_ada"][0]
    sh["wada"] = _tile_cols(wada, [ar(t * 384, (t + 1) * 384) for t in range(32)])
    sh["gmix"] = _col(inp["norm_mix_g"][0])
    sh["gffn"] = _col(inp["norm_ffn_g"][0])
    sh["gfin"] = _col(inp["norm_final_g"])
    cw = inp["conv_w"][0]
    sh["convw"] = np.ascontiguousarray(cw.reshape(3, 8, 128).transpose(2, 1, 0)).reshape(128, 24)
    lp = inp["lb_param"]
    sh["lbp"] = np.ascontiguousarray(lp.reshape(2, 8, 128).transpose(2, 1, 0)).reshape(128, 16)
    sh["gn"] = np.ascontiguousarray(inp["gnorm_g"][0].reshape(128, 1))
    sh["w_conv"] = _tile_cols(w_in, [np.concatenate([ar(j * 128, (j + 1) * 128), 1024 + ar(j * 128, (j + 1) * 128),
                                                     2048 + ar(j * 128, (j + 1) * 128)]) for j in range(8)])
    sh["w_hq"] = _tile_cols(w_in, [np.concatenate([3072 + ar(j * 128, (j + 1) * 128), 4096 + ar(j * 128, (j + 1) * 128),
                                                   6144 + ar(j * 128, (j + 1) * 128)]) for j in range(8)])
    sh["w_f"] = _tile_cols(w_in, [4096 + ar(j * 128, (j + 1) * 128) for j in range(8)])
    wv = w_in[:, 5120:6144]
    sh["w_v"] = np.stack([_tile_cols(wv[kh * 1024:(kh + 1) * 1024], [ar(cb * 512, (cb + 1) * 512)])[0]
                          for cb in range(2) for kh in range(2)])
    wg = _tile_cols(w_in, [np.concatenate([7168 + ar(j * 128, (j + 1) * 128), 9216 + ar(j * 128, (j + 1) * 128)])
                           for j in range(16)])
    wyo = np.concatenate([inp["w_conv_out"][0], inp["w_hgrn_out"][0]], axis=0)
    wy = _tile_cols(wyo, [ar(j * 128, (j + 1) * 128) for j in range(16)])
    sh["w_gm"] = np.ascontiguousarray(np.concatenate([wg, wy], axis=2))
    sh["w_o"] = _tile_cols(inp["w_o"][0], [ar(j * 256, (j + 1) * 256) for j in range(8)])
    wgate, wup = inp["w_ffn_gate"][0], inp["w_ffn_up"][0]
    wgu = np.concatenate([wgate.reshape(D, 44, 128), wup.reshape(D, 44, 128)], axis=2).reshape(D, 44 * 256)
    sh["w_gu"] = _tile_cols(wgu, [ar(j * 256, (j + 1) * 256) for j in range(44)])
    wd = inp["w_ffn_down"][0]
    sh["w_dn"] = np.stack([_tile_cols(wd[qq * 1408:(qq + 1) * 1408], [ar(cb * 512, (cb + 1) * 512)])[0]
                           for qq in range(4) for cb in range(4)])
    return sh


def _in_maps(inp):
    sh = _prep_shared({k: np.asarray(v, dtype=np.float32) for k, v in inp.items()})
    x = np.asarray(inp["x"], dtype=np.float32)
    c = np.asarray(inp["c"], dtype=np.float32)
    bada = _col(np.asarray(inp["b_ada"], dtype=np.float32)[0])
    zeros = np.zeros((2048, 2048), np.float32)
    maps = []
    for core in range(8):
        b, half = core // 2, core % 2
        m = dict(sh)
        m["xo"] = np.ascontiguousarray(x[b, half * 2048:(half + 1) * 2048])
        m["xp"] = np.ascontiguousarray(x[b, 0:2048]) if half == 1 else zeros
        m["flag"] = np.full((128, 1), float(half), np.float32)
        m["ccol"] = _col(c[b])
        m["badacol"] = bada
        maps.append(m)
    return maps


def kernel(**inputs):
    maps = _in_maps(inputs)
    if "nc" not in _CACHE:
        _CACHE["nc"] = build_nc()
    res = run_bass_kernel_spmd(_CACHE["nc"], maps, core_ids=list(range(8)))
    out = np.empty((4, 4096, 2048), np.float32)
    for core in range(8):
        b, half = core // 2, core % 2
        out[b, half * 2048:(half + 1) * 2048] = res.results[core]["out"]
    return out
```
